# Optimizing a Trainium2 kernel written in Bass

```python
import math, functools
import jax, jax.numpy as jnp
from jax import lax
import numpy as np

D_MODEL = 1024
BATCH = 4
SEQ = 8192
DEPTH = 1
DEC_BATCH = 2
DEC_SEQ = 16384
PAST_LEN = 128

HG_HEADS = 4
HG_HEAD_DIM = 128
HG_WIDTH = HG_HEADS * HG_HEAD_DIM
HG_CHUNK = 64
MLA_HEADS = 8
Q_LORA = 256
KV_LORA = 256
QK_NOPE = 64
QK_ROPE = 32
V_HEAD = 64
QK_HEAD = QK_NOPE + QK_ROPE
MLA_WIDTH = MLA_HEADS * V_HEAD
Q_BLOCK = 128
ROPE_THETA = 10000.0
N_BRANCH = 2
BRANCH_WIDTH = 512
D_FF = 2816
EPS = 1e-6
SPLIT_SIZES = (HG_WIDTH, HG_WIDTH, HG_WIDTH, HG_WIDTH, HG_WIDTH,
               Q_LORA, KV_LORA, QK_ROPE,
               N_BRANCH * D_MODEL)
IN_COLS = sum(SPLIT_SIZES)

kernel_name = "hgrn2_mla_parallel_encoder"


def _rmsnorm(x, g):
    xf = x.astype(jnp.float32)
    y = xf * lax.rsqrt(jnp.mean(xf * xf, axis=-1, keepdims=True) + EPS)
    return (y * g.astype(jnp.float32)).astype(x.dtype)


def _rope(x):
    S, d = x.shape[1], x.shape[-1]
    inv = ROPE_THETA ** (-jnp.arange(0, d, 2, dtype=jnp.float32) / d)
    ang = jnp.arange(S, dtype=jnp.float32)[:, None] * inv[None, :]
    cos = jnp.cos(ang)[None, :, None, :]
    sin = jnp.sin(ang)[None, :, None, :]
    xf = x.astype(jnp.float32)
    x1, x2 = xf[..., : d // 2], xf[..., d // 2:]
    out = jnp.concatenate([x1 * cos - x2 * sin, x2 * cos + x1 * sin], axis=-1)
    return out.astype(x.dtype)


def _hgrn2_scan(q, v, g_log):
    B, S, H, D = q.shape
    C = HG_CHUNK
    nc = S // C
    k = -jnp.expm1(g_log)

    def to_chunks(t):
        return t.reshape(B, nc, C, H, D).transpose(1, 0, 3, 2, 4)

    qc, kc, vc, gc = to_chunks(q), to_chunks(k), to_chunks(v), to_chunks(g_log)
    b = jnp.cumsum(gc, axis=-2)
    b_last = b[..., -1:, :]
    ref = b[..., C // 2 - 1: C // 2, :]
    q_in = qc * jnp.exp(b - ref)
    k_in = kc * jnp.exp(ref - b)
    att = jnp.einsum('nbhtd,nbhsd->nbhts', q_in, k_in)
    mask = jnp.tril(jnp.ones((C, C), dtype=bool))
    att = jnp.where(mask, att, 0.0)
    o_intra = jnp.einsum('nbhts,nbhse->nbhte', att, vc)
    q_inter = qc * jnp.exp(b)
    k_state = kc * jnp.exp(b_last - b)
    decay = jnp.exp(b_last[..., 0, :])

    def step(state, xs):
        qi, ks, vs, dec = xs
        o = jnp.einsum('bhtd,bhde->bhte', qi, state)
        state = state * dec[..., None] + jnp.einsum('bhsd,bhse->bhde', ks, vs)
        return state, o

    s0 = jnp.zeros((B, H, D, D), jnp.float32)
    _, o_inter = lax.scan(step, s0, (q_inter, k_state, vc, decay))
    o = o_intra + o_inter
    return o.transpose(1, 0, 3, 2, 4).reshape(B, S, H, D)


def _mla_attention(q, k, v):
    B, S, H, _ = q.shape
    nq = S // Q_BLOCK
    qb = q.reshape(B, nq, Q_BLOCK, H, QK_HEAD).transpose(1, 0, 2, 3, 4)
    scale = QK_HEAD ** -0.5

    def block(qi):
        s = jnp.einsum('bqhd,bkhd->bhqk', qi, k).astype(jnp.float32) * scale
        p = jax.nn.softmax(s, axis=-1).astype(v.dtype)
        return jnp.einsum('bhqk,bkhe->bqhe', p, v)

    o = lax.map(block, qb)
    return o.transpose(1, 0, 2, 3, 4).reshape(B, S, H, V_HEAD)


def _trunk(x, g_mix, w_in, lb_param, g_onorm, g_qa, w_uq, g_kva, w_ukv,
           w_branch, w_out, g_ffn, w_gate_up, w_down, g_final):
    B, S, _ = x.shape
    f32 = jnp.float32
    lower_bounds = jnp.cumsum(jax.nn.softmax(lb_param.astype(f32), axis=0), axis=0)
    split_pts = [int(v) for v in np.cumsum(SPLIT_SIZES)[:-1]]
    for l in range(DEPTH):
        h = _rmsnorm(x, g_mix[l])
        proj = h @ w_in[l]
        p_q, p_i, p_ff, p_fb, p_g, p_qa, p_kva, p_kr, p_gates = jnp.split(proj, split_pts, axis=-1)

        hd = (B, S, HG_HEADS, HG_HEAD_DIM)
        hq = jax.nn.silu(p_q.astype(f32)).reshape(hd)
        hi = p_i.astype(f32).reshape(hd)
        lb = lower_bounds[l]
        g_fw = jnp.log(lb[0] + (1.0 - lb[0]) * jax.nn.sigmoid(p_ff.astype(f32))).reshape(hd)
        g_bw = jnp.log(lb[1] + (1.0 - lb[1]) * jax.nn.sigmoid(p_fb.astype(f32))).reshape(hd)
        o_fw = _hgrn2_scan(hq, hi, g_fw)
        o_bw = jnp.flip(_hgrn2_scan(jnp.flip(hq, 1), jnp.flip(hi, 1), jnp.flip(g_bw, 1)), 1)
        o_h = _rmsnorm(o_fw + o_bw, g_onorm[l].reshape(HG_HEADS, HG_HEAD_DIM))
        o_h = (o_h * jax.nn.silu(p_g.astype(f32)).reshape(hd)).reshape(B, S, HG_WIDTH).astype(x.dtype)

        cq = _rmsnorm(p_qa, g_qa[l])
        q = jnp.einsum('bsr,rhd->bshd', cq, w_uq[l])
        ckv = _rmsnorm(p_kva, g_kva[l])
        kv = jnp.einsum('bsr,rhd->bshd', ckv, w_ukv[l])
        k_nope, v = kv[..., :QK_NOPE], kv[..., QK_NOPE:]
        k_pe = _rope(p_kr[:, :, None, :])
        q_full = jnp.concatenate([q[..., :QK_NOPE], _rope(q[..., QK_NOPE:])], axis=-1)
        k_full = jnp.concatenate(
            [k_nope, jnp.broadcast_to(k_pe, (B, S, MLA_HEADS, QK_ROPE))], axis=-1)
        o_m = _mla_attention(q_full, k_full, v).reshape(B, S, MLA_WIDTH)

        gates = jax.nn.sigmoid(p_gates.astype(f32)).reshape(B, S, N_BRANCH, D_MODEL).astype(x.dtype)
        branches = jnp.stack([o_h, o_m], axis=2)
        bproj = jnp.einsum('bsnk,nkd->bsnd', branches, w_branch[l])
        merged = jnp.sum(gates * bproj, axis=2)
        x = x + merged @ w_out[l]

        h2 = _rmsnorm(x, g_ffn[l])
        gu = h2 @ w_gate_up[l]
        gate, up = gu[..., :D_FF], gu[..., D_FF:]
        x = x + (jax.nn.silu(gate) * up) @ w_down[l]
    return _rmsnorm(x, g_final)


def setup_inputs(seed: int = 0) -> dict:
    key = jax.random.key(seed)
    ks = jax.random.split(key, 20)
    nrm = lambda k, shape, fan: jax.random.normal(k, shape, jnp.float32) * (fan ** -0.5)
    gain = lambda k, shape: 1.0 + 0.02 * jax.random.normal(k, shape, jnp.float32)
    return {
        "x_prompt": jax.random.normal(ks[0], (BATCH, SEQ, D_MODEL), jnp.float32),
        "x_sample": jax.random.normal(ks[1], (DEC_BATCH, DEC_SEQ, D_MODEL), jnp.float32),
        "g_mix": gain(ks[2], (DEPTH, D_MODEL)),
        "w_in": nrm(ks[3], (DEPTH, D_MODEL, IN_COLS), D_MODEL),
        "lb_param": 0.1 * jax.random.normal(ks[4], (DEPTH + 1, 2, HG_WIDTH), jnp.float32),
        "g_onorm": gain(ks[5], (DEPTH, HG_WIDTH)),
        "g_qa": gain(ks[6], (DEPTH, Q_LORA)),
        "w_uq": nrm(ks[7], (DEPTH, Q_LORA, MLA_HEADS, QK_HEAD), Q_LORA),
        "g_kva": gain(ks[8], (DEPTH, KV_LORA)),
        "w_ukv": nrm(ks[9], (DEPTH, KV_LORA, MLA_HEADS, QK_NOPE + V_HEAD), KV_LORA),
        "w_branch": nrm(ks[10], (DEPTH, N_BRANCH, BRANCH_WIDTH, D_MODEL), BRANCH_WIDTH),
        "w_out": nrm(ks[11], (DEPTH, D_MODEL, D_MODEL), D_MODEL),
        "g_ffn": gain(ks[12], (DEPTH, D_MODEL)),
        "w_gate_up": nrm(ks[13], (DEPTH, D_MODEL, 2 * D_FF), D_MODEL),
        "w_down": nrm(ks[14], (DEPTH, D_FF, D_MODEL), D_FF),
        "g_final": gain(ks[15], (D_MODEL,)),
    }


def reference(x_prompt, x_sample, g_mix, w_in, lb_param, g_onorm, g_qa, w_uq, g_kva, w_ukv,
              w_branch, w_out, g_ffn, w_gate_up, w_down, g_final):
    y_prompt = _trunk(x_prompt, g_mix, w_in, lb_param, g_onorm, g_qa, w_uq, g_kva, w_ukv,
                      w_branch, w_out, g_ffn, w_gate_up, w_down, g_final)
    y_sample = _trunk(x_sample, g_mix, w_in, lb_param, g_onorm, g_qa, w_uq, g_kva, w_ukv,
                      w_branch, w_out, g_ffn, w_gate_up, w_down, g_final)
    return (y_prompt, y_sample)
```

```python
from contextlib import ExitStack
import numpy as np
import ml_dtypes
import concourse.bass as bass
import concourse.mybir as mybir
from concourse.bass_utils import run_bass_kernel_spmd

F32 = mybir.dt.float32
BF16 = mybir.dt.bfloat16
AF = mybir.ActivationFunctionType
ALU = mybir.AluOpType

D = 1024
NCTX = 16384
NOWN = 8192
NTOK = NCTX + NOWN
EPS = 1e-6
DFF = 2816


class Sched:
    def __init__(self, nc, stack, n_dma_sems=24):
        self.nc = nc
        self.eng = {"pe": nc.tensor, "act": nc.scalar, "dve": nc.vector, "pool": nc.gpsimd, "sp": nc.sync}
        self.sem = {}
        self.cnt = {}
        for k in ["pe", "act", "dve", "pool"]:
            self.sem[k] = stack.enter_context(nc.semaphore("s_" + k))
            self.cnt[k] = 0
        self.dma_sems = [stack.enter_context(nc.semaphore("s_dma%d" % i)) for i in range(n_dma_sems)]
        self.dma_cnt = [0] * n_dma_sems
        self.dma_rr = 0
        self.dma_rr_sw = 0
        self.seen = {k: {} for k in self.eng}
        self.lastw = {}
        self.readers = {}
        self.ninst = 0

    def _semobj(self, key):
        return self.sem[key] if isinstance(key, str) else self.dma_sems[key]

    def _wait(self, engname, dep):
        key, val = dep
        if key == engname and engname == "pe":
            return
        cur = self.seen[engname].get(key, 0)
        if cur >= val:
            return
        self.eng[engname].wait_ge(self._semobj(key), val)
        self.seen[engname][key] = val

    def _deps(self, engname, reads, writes):
        best = {}
        for r in reads:
            ev = self.lastw.get(r)
            if ev is not None and best.get(ev[0], 0) < ev[1]:
                best[ev[0]] = ev[1]
        for w in writes:
            ev = self.lastw.get(w)
            if ev is not None and best.get(ev[0], 0) < ev[1]:
                best[ev[0]] = ev[1]
            for ev in self.readers.get(w, ()):
                if best.get(ev[0], 0) < ev[1]:
                    best[ev[0]] = ev[1]
        for k, v in best.items():
            self._wait(engname, (k, v))

    def _record(self, ev, reads, writes):
        for r in reads:
            self.readers.setdefault(r, []).append(ev)
        for w in writes:
            self.lastw[w] = ev
            self.readers[w] = []

    def op(self, engname, fn, reads=(), writes=()):
        self._deps(engname, reads, writes)
        inst = fn(self.eng[engname])
        self.cnt[engname] += 1
        inst.then_inc(self.sem[engname], 1)
        ev = (engname, self.cnt[engname])
        self._record(ev, reads, writes)
        self.ninst += 1
        return ev

    def pe(self, fns, reads=(), writes=()):
        self._deps("pe", reads, writes)
        inst = None
        for fn in fns:
            inst = fn(self.eng["pe"])
            self.ninst += 1
        self.cnt["pe"] += 1
        inst.then_inc(self.sem["pe"], 1)
        ev = ("pe", self.cnt["pe"])
        self._record(ev, reads, writes)
        return ev

    def dma(self, qname, out, in_, reads=(), writes=(), **kw):
        half = len(self.dma_sems) // 2
        if qname == "pool":
            i = half + self.dma_rr_sw
            self.dma_rr_sw = (self.dma_rr_sw + 1) % (len(self.dma_sems) - half)
        else:
            i = self.dma_rr
            self.dma_rr = (self.dma_rr + 1) % half
        if self.dma_cnt[i] > 0:
            self._wait(qname, (i, self.dma_cnt[i]))
        self._deps(qname, reads, writes)
        self.dma_cnt[i] += 16
        self.eng[qname].dma_start(out=out, in_=in_, **kw).then_inc(self.dma_sems[i], 16)
        ev = (i, self.dma_cnt[i])
        self._record(ev, reads, writes)
        self.ninst += 1
        return ev

    def barrier(self):
        evs = [(k, self.cnt[k]) for k in ["pe", "act", "dve", "pool"] if self.cnt[k] > 0]
        evs += [(i, c) for i, c in enumerate(self.dma_cnt) if c > 0]
        for e in self.eng:
            for ev in evs:
                self._wait(e, ev)
        self.lastw = {}
        self.readers = {}

    def finish(self):
        for i, c in enumerate(self.dma_cnt):
            if c > 0:
                self._wait("sp", (i, c))


class Rot:
    def __init__(self, aps, name):
        self.aps = aps
        self.name = name
        self.i = 0

    def next(self):
        k = self.i % len(self.aps)
        self.i += 1
        return self.aps[k], (self.name, k)


def build_nc(debug=0, stop_after=99):
    nc = bass.Bass("TRN2", target_bir_lowering=False)
    dt_in = lambda n, s, d=F32: nc.dram_tensor(n, list(s), d, kind="ExternalInput").ap()
    x_all = dt_in("x_all", [NTOK, D])
    g_mix = dt_in("g_mix", [1, D])
    w_kv = dt_in("w_kv", [D, 288])
    w_q = dt_in("w_q", [D, 256])
    w_ctx = dt_in("w_ctx", [4, D, 1024])
    w_hg = dt_in("w_hg", [D, 2560])
    w_gates = dt_in("w_gates", [D, 2048])
    w_uq = dt_in("w_uq", [256, 768])
    w_uk = dt_in("w_uk", [256, 512])
    w_uv = dt_in("w_uv", [256, 512])
    lbp_ctx = dt_in("lbp_ctx", [4, 2, 512])
    lbp_own = dt_in("lbp_own", [1, 2048])
    g_onorm = dt_in("g_onorm", [1, 512])
    g_qa = dt_in("g_qa", [128, 2])
    g_kva = dt_in("g_kva", [128, 2])
    w_branch = dt_in("w_branch", [2, 512, D])
    w_out = dt_in("w_out", [D, D])
    g_ffn = dt_in("g_ffn", [1, D])
    w_gu = dt_in("w_gu", [D, 2 * DFF])
    w_down = dt_in("w_down", [DFF, D])
    g_final = dt_in("g_final", [1, D])
    cmat = dt_in("cmat", [4, 128, 128])
    rope_q = dt_in("rope_q", [2, 96, NOWN])
    rope_k = dt_in("rope_k", [2, 32, NTOK])
    flags = dt_in("flags", [128, 16])
    lbc_in = dt_in("lbc", [128, 16])
    gon_in = dt_in("gon_c", [128, 4])
    maskrep_in = dt_in("maskrep", [128, 2, 4, 128])
    y = nc.dram_tensor("y", [NOWN, D], F32, kind="ExternalOutput").ap()

    def scr(name, shape, dtype, dbg):
        kind = "ExternalOutput" if (debug and dbg) else "Internal"
        return nc.dram_tensor(name, list(shape), dtype, kind=kind).ap()

    K_scr = scr("K_scr", [128, 4, NTOK], BF16, debug == 1)
    KPE_scr = scr("KPE_scr", [32, NTOK], BF16, debug == 1)
    V_scr = scr("V_scr", [8, 128, NTOK // 128, 65], BF16, debug == 1)
    Q_scr = scr("Q_scr", [8, 96, NOWN], BF16, debug == 1)
    HT_scr = scr("HT_scr", [128, 8, NOWN], BF16, debug == 1)
    SI_scr = scr("SI_scr", [4, 128, 512], F32, debug == 1)
    OH_scr = scr("OH_scr", [128, 4, NOWN], BF16, debug == 2)
    OM_scr = scr("OM_scr", [8, 64, NOWN], BF16, debug == 3)
    X1_scr = scr("X1_scr", [NOWN, D], F32, debug == 4)
    OFW_scr = scr("OFW_scr", [128, 4, NOWN], F32, False)

    with ExitStack() as st:
        S = Sched(nc, st)
        sb = lambda n, s, d=F32: st.enter_context(nc.sbuf_tensor(n, list(s), d))

        ident_f = sb("ident_f", [128, 128])
        ident = sb("ident", [128, 128], BF16)
        ones_bf = sb("ones_bf", [128, 128], BF16)
        cm = sb("cm", [128, 4, 128])
        chunkind = sb("chunkind", [128, 2])
        flg = sb("flg", [128, 16])
        negh = sb("negh", [128, 512])
        gkva_c = sb("gkva_c", [128, 2])
        gqa_c = sb("gqa_c", [128, 2])
        sAB = ExitStack()
        sbab = lambda n, s_, d=F32: sAB.enter_context(nc.sbuf_tensor(n, list(s_), d))
        gmix_bc = sbab("gmix_bc", [128, D])
        Sinit = sbab("Sinit", [128, 4, 512])

        S.op("pool", lambda e: e.memset(ident_f[:], 0.0), writes=["ident_f"])
        S.op("pool", lambda e: e.affine_select(out=ident_f[:], in_=ident_f[:], pattern=[[-1, 128]],
                                                compare_op=ALU.not_equal, fill=1.0, base=0,
                                                channel_multiplier=1), reads=["ident_f"], writes=["ident_f"])
        S.op("pool", lambda e: e.tensor_copy(out=ident[:], in_=ident_f[:]), reads=["ident_f"], writes=["ident"])
        S.op("pool", lambda e: e.memset(ones_bf[:], 1.0), writes=["ones_bf"])
        S.op("pool", lambda e: e.memset(negh[:], -0.5), writes=["negh"])
        S.op("pool", lambda e: e.memset(chunkind[:], 0.0), writes=["chunkind"])
        S.op("pool", lambda e: e.memset(chunkind[0:64, 0:1], 1.0), reads=["chunkind"], writes=["chunkind"])
        S.op("pool", lambda e: e.memset(chunkind[64:128, 1:2], 1.0), reads=["chunkind"], writes=["chunkind"])
        S.op("pool", lambda e: e.memset(Sinit[:], 0.0), writes=["Sinit"])
        S.dma("sp", cm[:], cmat.rearrange("m p t -> p m t"), writes=["cm"])
        S.dma("sp", flg[:], flags[:, :], writes=["flg"])
        S.dma("sp", gmix_bc[:], g_mix.broadcast_to([128, D]), writes=["gmix_bc"])
        S.dma("sp", gkva_c[:], g_kva[:, :], writes=["gkva_c"])
        S.dma("sp", gqa_c[:], g_qa[:, :], writes=["gqa_c"])


        def load_w(dst, src, kc, cols, dst_key, col0=0, scale=None, pp=128):
            assert scale is None
            per = max(1, 4096 // cols)
            k0 = 0
            while k0 < kc:
                n = min(per, kc - k0)
                S.dma("pool", dst[0:pp, k0:k0 + n, col0:col0 + cols],
                      src[k0 * pp:(k0 + n) * pp, :].rearrange("(k p) c -> p k c", p=pp), writes=[dst_key])
                k0 += n

        Wkv = sbab("Wkv", [128, 8, 320], BF16)
        Wuk = sbab("Wuk", [128, 2, 512], BF16)
        Wuv = sbab("Wuv", [128, 2, 512], BF16)
        load_w(Wkv, w_kv, 8, 288, "Wkv")
        S.op("act", lambda e: e.activation(out=Wkv[:, :, 288:304], in_=Wkv[:, :, 272:288], func=AF.Copy, scale=-1.0),
             reads=["Wkv"], writes=["Wkv"])
        S.op("act", lambda e: e.activation(out=Wkv[:, :, 304:320], in_=Wkv[:, :, 256:272], func=AF.Copy),
             reads=["Wkv"], writes=["Wkv"])
        load_w(Wuk, w_uk, 2, 512, "Wuk")
        load_w(Wuv, w_uv, 2, 512, "Wuv")

        def rmsnorm_T(g, xt, hn, hT, hTkey, ptr_rot, ss, ms, rstd, junk):
            for t in range(4):
                S.op("act", lambda e, t=t: e.activation(out=junk[:], in_=xt[:, t, :], func=AF.Square,
                                                         accum_out=ss[:, t:t + 1]),
                     reads=["xt%d" % (g % 2)], writes=["junk", "ss"])
            S.op("dve", lambda e: e.tensor_scalar(out=ms[:], in0=ss[:], scalar1=1.0 / D, scalar2=EPS,
                                                  op0=ALU.mult, op1=ALU.add), reads=["ss"], writes=["ms"])
            S.op("pool", lambda e: e.tensor_tensor(out=rstd[:], in0=ms[:], in1=negh[:, 0:4], op=ALU.pow),
                 reads=["ms", "negh"], writes=["rstd"])
            for t in range(4):
                S.op("dve", lambda e, t=t: e.scalar_tensor_tensor(out=hn[:, t, :], in0=xt[:, t, :],
                                                                  scalar=rstd[:, t:t + 1], in1=gmix_bc[:],
                                                                  op0=ALU.mult, op1=ALU.mult),
                     reads=["xt%d" % (g % 2), "rstd", "gmix_bc"], writes=[("hn", t)])
            for t in range(4):
                ptr, pkey = ptr_rot.next()
                S.pe([lambda e, t=t, kc=kc, ptr=ptr: e.transpose(ptr[:, kc, :], hn[:, t, kc * 128:(kc + 1) * 128], ident[:])
                      for kc in range(8)], reads=[("hn", t), "ident"], writes=[pkey])
                S.op("act", lambda e, t=t, ptr=ptr: e.activation(out=hT[:, :, t * 128:(t + 1) * 128], in_=ptr[:],
                                                                  func=AF.Copy),
                     reads=[pkey], writes=[hTkey])

        x_v = x_all.rearrange("(g t p) d -> g p t d", t=4, p=128)

        ctx_groups = list(range(0, 8)) + list(range(16, 40))
        own_groups = list(range(8, 16)) + list(range(40, 48))

        with ExitStack() as sa:
            sba = lambda n, s, d=F32: sa.enter_context(nc.sbuf_tensor(n, list(s), d))
            xts = [sba("xt%d" % i, [128, 4, D]) for i in range(2)]
            hn = sba("hn", [128, 4, D], BF16)
            hTs = [sba("hT%d" % i, [128, 8, 512], BF16) for i in range(2)]
            junk = sba("junk", [128, D], BF16)
            ss = sba("ss", [128, 4]); ms = sba("ms", [128, 4]); rstd = sba("rstd", [128, 4])
            sq = sba("sq", [128, 2, 512], BF16)
            msk = sba("msk", [128, 512]); rk = sba("rk", [128, 512])
            ckvT = sba("ckvT", [128, 2, 512], BF16)
            rkt = sba("rkt", [32, 2, 512])
            t1 = sba("t1", [128, 512]); t2 = sba("t2", [128, 512])
            kpe = sba("kpe", [32, 512], BF16)
            kT = sba("kT", [128, 4, 512], BF16)
            vt = sba("vt", [128, 8, 4, 65], BF16)
            S.op("pool", lambda e: e.memset(vt[:], 1.0), writes=["vt"])
            ptrs = [sa.enter_context(nc.psum_tensor("ptr%d" % i, [128, 8, 128], BF16)) for i in range(2)]
            pbs = [sa.enter_context(nc.psum_tensor("pb%d" % i, [128, 512], F32)) for i in range(5)]
            ptr_rot = Rot(ptrs, "ptr")
            pb_rot = Rot(pbs, "pb")

            def kv_path(g, hT, hTkey):
                pk = []
                for blk in range(2):
                    pb, key = pb_rot.next()
                    S.pe([lambda e, kc=kc, blk=blk, pb=pb: e.matmul(pb[:], Wkv[:, kc, blk * 128:(blk + 1) * 128], hT[:, kc, :],
                                                                    start=(kc == 0), stop=(kc == 7)) for kc in range(8)],
                         reads=[hTkey, "Wkv"], writes=[key])
                    S.op("act", lambda e, blk=blk, pb=pb: e.activation(out=sq[:, blk, :], in_=pb[:], func=AF.Square),
                         reads=[key], writes=[("sq", blk)])
                    pk.append((pb, key))
                pss, skey = pb_rot.next()
                S.pe([lambda e, blk=blk: e.matmul(pss[:], ones_bf[:], sq[:, blk, :], start=(blk == 0), stop=(blk == 1))
                      for blk in range(2)], reads=[("sq", 0), ("sq", 1), "ones_bf"], writes=[skey])
                S.op("dve", lambda e: e.tensor_scalar(out=msk[:], in0=pss[:], scalar1=1.0 / 256, scalar2=EPS,
                                                      op0=ALU.mult, op1=ALU.add), reads=[skey], writes=["msk"])
                S.op("act", lambda e: e.activation(out=msk[:], in_=msk[:], func=AF.Ln), reads=["msk"], writes=["msk"])
                S.op("act", lambda e: e.activation(out=rk[:], in_=msk[:], func=AF.Exp, scale=-0.5), reads=["msk"], writes=["rk"])
                for blk in range(2):
                    pb, key = pk[blk]
                    S.op("dve", lambda e, blk=blk, pb=pb: e.scalar_tensor_tensor(
                        out=ckvT[:, blk, :], in0=pb[:], scalar=gkva_c[:, blk:blk + 1], in1=rk[:],
                        op0=ALU.mult, op1=ALU.mult), reads=[key, "rk", "gkva_c"], writes=["ckvT"])
                S.dma("sp", rkt[:], rope_k[:, :, g * 512:(g + 1) * 512].rearrange("c r t -> r c t"), writes=["rkt"])
                pkr, kkey = pb_rot.next()
                S.pe([lambda e, kc=kc: e.matmul(pkr[0:32, :], Wkv[:, kc, 256:288], hT[:, kc, :],
                                                start=(kc == 0), stop=(kc == 7)) for kc in range(8)],
                     reads=[hTkey, "Wkv"], writes=[kkey])
                pkrr, rkey = pb_rot.next()
                S.pe([lambda e, kc=kc: e.matmul(pkrr[0:32, :], Wkv[:, kc, 288:320], hT[:, kc, :],
                                                start=(kc == 0), stop=(kc == 7)) for kc in range(8)],
                     reads=[hTkey, "Wkv"], writes=[rkey])
                S.op("dve", lambda e: e.tensor_tensor(out=t1[0:32, :], in0=pkrr[0:32, :], in1=rkt[:, 1, :], op=ALU.mult),
                     reads=[rkey, "rkt"], writes=["t1"])
                S.op("dve", lambda e: e.tensor_tensor(out=t2[0:32, :], in0=pkr[0:32, :], in1=rkt[:, 0, :], op=ALU.mult),
                     reads=[kkey, "rkt"], writes=["t2"])
                S.op("dve", lambda e: e.tensor_tensor(out=kpe[:], in0=t1[0:32, :], in1=t2[0:32, :], op=ALU.add),
                     reads=["t1", "t2"], writes=["kpe"])
                S.dma("pool", KPE_scr[:, g * 512:(g + 1) * 512], kpe[:], reads=["kpe"], writes=["KPE_scr"])
                for pr in range(4):
                    pb, key = pb_rot.next()
                    S.pe([lambda e, kc=kc, pr=pr, pb=pb: e.matmul(pb[:], Wuk[:, kc, pr * 128:(pr + 1) * 128], ckvT[:, kc, :],
                                                                  start=(kc == 0), stop=(kc == 1)) for kc in range(2)],
                         reads=["ckvT", "Wuk"], writes=[key])
                    S.op("act", lambda e, pr=pr, pb=pb: e.activation(out=kT[:, pr, :], in_=pb[:], func=AF.Copy),
                         reads=[key], writes=["kT"])
                S.dma("pool", K_scr[:, :, g * 512:(g + 1) * 512], kT[:], reads=["kT"], writes=["K_scr"])
                for t in range(4):
                    pb, key = pb_rot.next()
                    S.pe([lambda e, kc=kc, t=t, pb=pb: e.matmul(pb[:], ckvT[:, kc, t * 128:(t + 1) * 128], Wuv[:, kc, :],
                                                                start=(kc == 0), stop=(kc == 1)) for kc in range(2)],
                         reads=["ckvT", "Wuv"], writes=[key])
                    S.op("act", lambda e, t=t, pb=pb: e.activation(out=vt[:, :, t, 0:64], in_=pb[:].rearrange("p (h e) -> p h e", e=64), func=AF.Copy),
                         reads=[key], writes=["vt"])
                S.dma("pool", V_scr[:, :, g * 4:(g + 1) * 4, :].rearrange("h p t e -> p h (t e)"),
                      vt[:].rearrange("p h t e -> p h (t e)"), reads=["vt"], writes=["V_scr"])

            with ExitStack() as s1:
                sb1 = lambda n, s, d=F32: s1.enter_context(nc.sbuf_tensor(n, list(s), d))
                Wctx = sb1("Wctx", [128, 8, 1024], BF16)
                lbt = sb1("lbt", [128, 2, 512]); lbd = sb1("lbd", [128, 512])
                oh_bc = sb1("oh_bc", [128, 512]); c_bc = sb1("c_bc", [128, 512])
                vb4 = [sb1("vb4_%d" % i, [128, 4, 512], BF16) for i in range(2)]
                ks4 = [sb1("ks4_%d" % i, [128, 4, 512], BF16) for i in range(2)]
                TA = sb1("TA", [128, 4, 512]); TC = sb1("TC", [128, 4, 512])
                TB = [sb1("TB%d" % i, [128, 4, 512]) for i in range(2)]
                dec4 = [sb1("dec4_%d" % i, [128, 32]) for i in range(2)]
                St = sb1("St", [128, 512])
                pdec = s1.enter_context(nc.psum_tensor("pdec", [128, 32], F32))
                S.op("pool", lambda e: e.memset(St[:], 0.0), writes=["St"])
                def gid(slot_, gi_):
                    return (slot_ * 8 + gi_) if slot_ == 0 else (16 + (slot_ - 1) * 8 + gi_)
                glist = [(slot_, gi_, gid(slot_, gi_)) for slot_ in range(4) for gi_ in range(8)]
                NG = len(glist)

                def front(n):
                    g = glist[n][2]
                    S.dma("sp", xts[g % 2][:], x_v[g], writes=["xt%d" % (g % 2)])
                    rmsnorm_T(g, xts[g % 2], hn, hTs[g % 2], "hT%d" % (g % 2), ptr_rot, ss, ms, rstd, junk)

                def kvp(n):
                    g = glist[n][2]
                    kv_path(g, hTs[g % 2], "hT%d" % (g % 2))

                def slot_start(slot):
                    load_w(Wctx, w_ctx[slot], 8, 1024, "Wctx")
                    S.dma("sp", lbt[:], lbp_ctx[slot:slot + 1].rearrange("o l f -> o (l f)").broadcast_to([128, 1024])
                          .rearrange("p (l f) -> p l f", l=2), writes=["lbt"])
                    S.op("dve", lambda e: e.tensor_tensor(out=lbd[:], in0=lbt[:, 0, :], in1=lbt[:, 1, :], op=ALU.subtract),
                         reads=["lbt"], writes=["lbd"])
                    S.op("act", lambda e: e.activation(out=lbd[:], in_=lbd[:], func=AF.Tanh, scale=0.5),
                         reads=["lbd"], writes=["lbd"])
                    S.op("dve", lambda e: e.tensor_scalar(out=oh_bc[:], in0=lbd[:], scalar1=-0.25, scalar2=0.25,
                                                          op0=ALU.mult, op1=ALU.add), reads=["lbd"], writes=["oh_bc"])
                    S.op("dve", lambda e: e.tensor_scalar(out=c_bc[:], in0=lbd[:], scalar1=0.25, scalar2=0.75,
                                                          op0=ALU.mult, op1=ALU.add), reads=["lbd"], writes=["c_bc"])
                    if slot >= 2:
                        S.op("dve", lambda e: e.tensor_scalar(out=St[:], in0=St[:], scalar1=flg[:, 2 + slot:3 + slot],
                                                              scalar2=None, op0=ALU.mult),
                             reads=["St", "flg"], writes=["St"])

                def slot_end(slot):
                    if slot == 0:
                        for d_ in range(2):
                            S.op("dve", lambda e, d_=d_: e.tensor_scalar(out=Sinit[:, d_, :], in0=St[:], scalar1=flg[:, d_:d_ + 1],
                                                                         scalar2=None, op0=ALU.mult),
                                 reads=["St", "flg"], writes=["Sinit"])
                        S.op("pool", lambda e: e.memset(St[:], 0.0), reads=["St"], writes=["St"])
                    else:
                        S.op("dve", lambda e: e.scalar_tensor_tensor(
                            out=Sinit[:, 2, :], in0=St[:], scalar=flg[:, 5 + slot:6 + slot], in1=Sinit[:, 2, :],
                            op0=ALU.mult, op1=ALU.add), reads=["St", "flg", "Sinit"], writes=["Sinit"])
                        if slot == 3:
                            S.op("dve", lambda e: e.tensor_scalar(out=Sinit[:, 3, :], in0=St[:], scalar1=flg[:, 9:10],
                                                                  scalar2=None, op0=ALU.mult),
                                 reads=["St", "flg"], writes=["Sinit"])

                oh4 = oh_bc[:].unsqueeze(1).broadcast_to([128, 4, 512])
                c4 = c_bc[:].unsqueeze(1).broadcast_to([128, 4, 512])

                def T1(n):
                    g = glist[n][2]
                    p = n % 2
                    hT = hTs[g % 2]; hTkey = "hT%d" % (g % 2)
                    for t in range(4):
                        pi, ikey = pb_rot.next()
                        S.pe([lambda e, kc=kc: e.matmul(pi[:], hT[:, kc, t * 128:(t + 1) * 128], Wctx[:, kc, 0:512],
                                                        start=(kc == 0), stop=(kc == 7)) for kc in range(8)],
                             reads=[hTkey, "Wctx"], writes=[ikey])
                        S.op("act", lambda e: e.activation(out=vb4[p][:, t, :], in_=pi[:], func=AF.Copy),
                             reads=[ikey], writes=[("vb4", p)])
                        pf, fkey = pb_rot.next()
                        S.pe([lambda e, kc=kc: e.matmul(pf[:], hT[:, kc, t * 128:(t + 1) * 128], Wctx[:, kc, 512:1024],
                                                        start=(kc == 0), stop=(kc == 7)) for kc in range(8)],
                             reads=[hTkey, "Wctx"], writes=[fkey])
                        S.op("act", lambda e: e.activation(out=TA[:, t, :], in_=pf[:], func=AF.Tanh, scale=0.5),
                             reads=[fkey], writes=["TA"])
                    S.op("dve", lambda e: e.tensor_tensor(out=TA[:], in0=TA[:], in1=oh4, op=ALU.mult),
                         reads=["TA", "oh_bc"], writes=["TA"])
                    S.op("dve", lambda e: e.tensor_tensor(out=TB[p][:], in0=TA[:], in1=c4, op=ALU.add),
                         reads=["TA", "c_bc"], writes=[("TB", p)])
                    S.op("dve", lambda e: e.tensor_tensor(out=TC[:], in0=oh4, in1=TA[:], op=ALU.subtract),
                         reads=["TA", "oh_bc"], writes=["TC"])

                def T1b(n):
                    p = n % 2
                    S.op("act", lambda e: e.activation(out=TB[p][:], in_=TB[p][:], func=AF.Ln),
                         reads=[("TB", p)], writes=[("TB", p)])

                def T2(n):
                    p = n % 2
                    gl4 = TB[p]
                    for t in range(4):
                        pe2, ekey = pb_rot.next()
                        S.pe([lambda e: e.matmul(pe2[:], cm[:, 2, :], gl4[:, t, :], start=True, stop=True)],
                             reads=[("TB", p), "cm"], writes=[ekey])
                        S.op("act", lambda e: e.activation(out=TA[:, t, :], in_=pe2[:], func=AF.Exp),
                             reads=[ekey], writes=["TA"])
                    S.pe([lambda e, t=t, h=h: e.matmul(pdec[:, t * 8 + 2 * h:t * 8 + 2 * h + 2], gl4[:, t, h * 128:(h + 1) * 128], chunkind[:],
                                                       start=True, stop=True) for t in range(4) for h in range(4)],
                         reads=[("TB", p), "chunkind"], writes=["pdec"])
                    S.op("act", lambda e: e.activation(out=dec4[p][:], in_=pdec[:], func=AF.Exp),
                         reads=["pdec"], writes=[("dec4", p)])
                    S.op("dve", lambda e: e.tensor_tensor(out=ks4[p][:], in0=TC[:], in1=TA[:], op=ALU.mult),
                         reads=["TC", "TA"], writes=[("ks4", p)])

                def U(n):
                    p = n % 2
                    for t in range(4):
                        for c in range(2):
                            pkv, vkey = pb_rot.next()
                            S.pe([lambda e, h=h: e.matmul(
                                pkv[:, h * 128:(h + 1) * 128], ks4[p][c * 64:(c + 1) * 64, t, h * 128:(h + 1) * 128],
                                vb4[p][c * 64:(c + 1) * 64, t, h * 128:(h + 1) * 128], start=True, stop=True) for h in range(4)],
                                reads=[("ks4", p), ("vb4", p)], writes=[vkey])
                            for h in range(4):
                                S.op("dve", lambda e, h=h: e.scalar_tensor_tensor(
                                    out=St[:, h * 128:(h + 1) * 128], in0=St[:, h * 128:(h + 1) * 128],
                                    scalar=dec4[p][:, t * 8 + 2 * h + c:t * 8 + 2 * h + c + 1], in1=pkv[:, h * 128:(h + 1) * 128],
                                    op0=ALU.mult, op1=ALU.add), reads=["St", ("dec4", p), vkey], writes=["St"])

                front(0); front(1); kvp(0)
                for n in range(NG):
                    slot, gi, g = glist[n]
                    if gi == 0:
                        slot_start(slot)
                    T1(n)
                    if n + 2 < NG:
                        front(n + 2)
                    T1b(n)
                    if n + 1 < NG:
                        kvp(n + 1)
                    T2(n)
                    U(n)
                    if gi == 7:
                        slot_end(slot)
                S.dma("pool", SI_scr.rearrange("s p f -> p s f"), Sinit[:], reads=["Sinit"], writes=["SI_scr"])
                S.barrier()

            with ExitStack() as s2:
                sb2 = lambda n, s, d=F32: s2.enter_context(nc.sbuf_tensor(n, list(s), d))
                Wq = sb2("Wq", [128, 8, 256], BF16)
                Wuq = sb2("Wuq", [128, 2, 768], BF16)
                Wuqr = sb2("Wuqr", [128, 2, 768], BF16)
                load_w(Wq, w_q, 8, 256, "Wq")
                load_w(Wuq, w_uq, 2, 768, "Wuq")
                S.op("pool", lambda e: e.memset(Wuqr[:], 0.0), writes=["Wuqr"])
                Wuq4 = Wuq[:].rearrange("p k (h c) -> p k h c", c=96)
                Wuqr4 = Wuqr[:].rearrange("p k (h c) -> p k h c", c=96)
                for kc in range(2):
                    S.op("act", lambda e, kc=kc: e.activation(out=Wuqr4[:, kc, :, 64:80], in_=Wuq4[:, kc, :, 80:96], func=AF.Copy,
                                                              scale=-1.0), reads=["Wuq", "Wuqr"], writes=["Wuqr"])
                    S.op("act", lambda e, kc=kc: e.activation(out=Wuqr4[:, kc, :, 80:96], in_=Wuq4[:, kc, :, 64:80], func=AF.Copy),
                         reads=["Wuq", "Wuqr"], writes=["Wuqr"])
                cqT = sb2("cqT", [128, 2, 512], BF16)
                rqt = sb2("rqt", [96, 2, 512])
                qT = sb2("qT", [96, 8, 512], BF16)
                def front2(g):
                    S.dma("sp", xts[g % 2][:], x_v[g], writes=["xt%d" % (g % 2)])
                    rmsnorm_T(g, xts[g % 2], hn, hTs[g % 2], "hT%d" % (g % 2), ptr_rot, ss, ms, rstd, junk)
                front2(own_groups[0])
                for oi, g in enumerate(own_groups):
                    xt = xts[g % 2]; hT = hTs[g % 2]; hTkey = "hT%d" % (g % 2)
                    if oi + 1 < len(own_groups):
                        front2(own_groups[oi + 1])
                    S.dma("pool", HT_scr[:, :, oi * 512:(oi + 1) * 512], hT[:], reads=[hTkey], writes=["HT_scr"])
                    kv_path(g, hT, hTkey)
                    pk = []
                    for blk in range(2):
                        pb, key = pb_rot.next()
                        S.pe([lambda e, kc=kc, blk=blk, pb=pb: e.matmul(pb[:], Wq[:, kc, blk * 128:(blk + 1) * 128], hT[:, kc, :],
                                                                        start=(kc == 0), stop=(kc == 7)) for kc in range(8)],
                             reads=[hTkey, "Wq"], writes=[key])
                        S.op("act", lambda e, blk=blk, pb=pb: e.activation(out=sq[:, blk, :], in_=pb[:], func=AF.Square),
                             reads=[key], writes=[("sq", blk)])
                        pk.append((pb, key))
                    pss, skey = pb_rot.next()
                    S.pe([lambda e, blk=blk: e.matmul(pss[:], ones_bf[:], sq[:, blk, :], start=(blk == 0), stop=(blk == 1))
                          for blk in range(2)], reads=[("sq", 0), ("sq", 1), "ones_bf"], writes=[skey])
                    S.op("dve", lambda e: e.tensor_scalar(out=msk[:], in0=pss[:], scalar1=1.0 / 256, scalar2=EPS,
                                                          op0=ALU.mult, op1=ALU.add), reads=[skey], writes=["msk"])
                    S.op("act", lambda e: e.activation(out=msk[:], in_=msk[:], func=AF.Ln), reads=["msk"], writes=["msk"])
                    S.op("act", lambda e: e.activation(out=rk[:], in_=msk[:], func=AF.Exp, scale=-0.5), reads=["msk"], writes=["rk"])
                    for blk in range(2):
                        pb, key = pk[blk]
                        S.op("dve", lambda e, blk=blk, pb=pb: e.scalar_tensor_tensor(
                            out=cqT[:, blk, :], in0=pb[:], scalar=gqa_c[:, blk:blk + 1], in1=rk[:],
                            op0=ALU.mult, op1=ALU.mult), reads=[key, "rk", "gqa_c"], writes=["cqT"])
                    S.dma("sp", rqt[:], rope_q[:, :, oi * 512:(oi + 1) * 512].rearrange("c r t -> r c t"), writes=["rqt"])
                    for h in range(8):
                        qa, akey = pb_rot.next()
                        S.pe([lambda e, kc=kc, h=h, qa=qa: e.matmul(qa[0:96, :], Wuq[:, kc, h * 96:(h + 1) * 96], cqT[:, kc, :],
                                                                    start=(kc == 0), stop=(kc == 1)) for kc in range(2)],
                             reads=["cqT", "Wuq"], writes=[akey])
                        qb, bkey = pb_rot.next()
                        S.pe([lambda e, kc=kc, h=h, qb=qb: e.matmul(qb[0:96, :], Wuqr[:, kc, h * 96:(h + 1) * 96], cqT[:, kc, :],
                                                                    start=(kc == 0), stop=(kc == 1)) for kc in range(2)],
                             reads=["cqT", "Wuqr"], writes=[bkey])
                        S.op("dve", lambda e, qb=qb: e.tensor_tensor(out=t1[0:96, :], in0=qb[0:96, :], in1=rqt[:, 1, :], op=ALU.mult),
                             reads=[bkey, "rqt"], writes=["t1"])
                        S.op("dve", lambda e, qa=qa: e.tensor_tensor(out=t2[0:96, :], in0=qa[0:96, :], in1=rqt[:, 0, :], op=ALU.mult),
                             reads=[akey, "rqt"], writes=["t2"])
                        S.op("dve", lambda e, h=h: e.tensor_tensor(out=qT[:, h, :], in0=t1[0:96, :], in1=t2[0:96, :], op=ALU.add),
                             reads=["t1", "t2"], writes=["qT"])
                    S.dma("pool", Q_scr[:, :, oi * 512:(oi + 1) * 512].rearrange("h r t -> r h t"), qT[:],
                          reads=["qT"], writes=["Q_scr"])
                S.barrier()

        if stop_after <= 1:
            sAB.close()
            S.finish()
            return nc, S


        if stop_after >= 2:
          with ExitStack() as sbk:
            sbb = lambda n, s, d=F32: sbk.enter_context(nc.sbuf_tensor("B_" + n, list(s), d))
            Whg = sbb("Whg", [128, 8, 2560], BF16)
            load_w(Whg, w_hg, 8, 2560, "Whg")
            hTl = [sbb("hTl0", [128, 8, 512], BF16)] * 2
            oh_bc = sbb("oh_bc", [128, 2, 512]); c_bc = sbb("c_bc", [128, 2, 512])
            with nc.sbuf_tensor("B_lbt", [128, 2, 1024], F32) as lbt, nc.sbuf_tensor("B_lbd", [128, 1024], F32) as lbd:
                S.dma("sp", lbt[:], lbp_own.broadcast_to([128, 2048]).rearrange("p (l f) -> p l f", l=2), writes=["lbt"])
                S.op("dve", lambda e: e.tensor_tensor(out=lbd[:], in0=lbt[:, 0, :], in1=lbt[:, 1, :], op=ALU.subtract),
                     reads=["lbt"], writes=["lbd"])
                S.op("act", lambda e: e.activation(out=lbd[:], in_=lbd[:], func=AF.Tanh, scale=0.5), reads=["lbd"], writes=["lbd"])
                S.op("dve", lambda e: e.tensor_scalar(out=oh_bc[:].rearrange("p r f -> p (r f)"), in0=lbd[:], scalar1=-0.25, scalar2=0.25,
                                                      op0=ALU.mult, op1=ALU.add), reads=["lbd"], writes=["oh_bc"])
                S.op("dve", lambda e: e.tensor_scalar(out=c_bc[:].rearrange("p r f -> p (r f)"), in0=lbd[:], scalar1=0.25, scalar2=0.75,
                                                      op0=ALU.mult, op1=ALU.add), reads=["lbd"], writes=["c_bc"])
                S.barrier()
            lbc = sbb("lbc", [128, 16]); lbcd = sbb("lbcd", [128, 8])
            ohc = sbb("ohc", [128, 8]); nohc = sbb("nohc", [128, 8])
            gonh = sbb("gonh", [128, 4])
            maskrep = sbb("maskrep", [128, 2, 4, 128])
            qs = sbb("qs", [128, 4, 512]); kTt = sbb("kTt", [128, 4, 512])
            sg2s = [sbb("sg2_%d" % i, [128, 4, 512]) for i in range(2)]
            thq = sbb("thq", [128, 512])
            vbs = [sbb("vb%d" % i, [128, 512], BF16) for i in range(4)]; th = sbb("th", [128, 512]); Aa = sbb("Aa", [128, 512])
            ff = sbb("ff", [128, 512]); kks = [sbb("kk%d" % i, [128, 512]) for i in range(2)]
            gls = [sbb("gl%d" % i, [128, 512]) for i in range(2)]
            E2 = sbb("E2", [128, 512]); kss = [sbb("ks%d" % i, [128, 512], BF16) for i in range(3)]
            Eis = [sbb("Ei%d" % i, [128, 4, 128]) for i in range(3)]; Eni = sbb("Eni", [128, 4, 128])
            Qts = [sbb("Qt%d" % i, [128, 4, 128], BF16) for i in range(3)]; Kts = [sbb("Kt%d" % i, [128, 4, 128], BF16) for i in range(2)]
            attms = [sbb("attm%d" % i, [128, 4, 128], BF16) for i in range(2)]
            St = sbb("St", [128, 512]); Sbfs = [sbb("Sbf%d" % i, [128, 512], BF16) for i in range(2)]
            sctr = [0]
            ofwg = sbb("ofwg", [128, 4, 512]); ofwts = [sbb("ofwt%d" % i, [128, 4, 512]) for i in range(2)]
            osum = sbb("osum", [128, 4, 128]); o2 = sbb("o2", [128, 4, 128])
            sqo = sbb("sqo", [128, 4, 128], BF16); mso = sbb("mso", [128, 512]); rro = mso
            ohg = sbb("ohg", [128, 4, 512], BF16)
            pbs = [sbk.enter_context(nc.psum_tensor("pbB%d" % i, [128, 512], F32)) for i in range(4)]
            pb_rot = Rot(pbs, "pbB")
            pos_ = [sbk.enter_context(nc.psum_tensor("poB%d" % i, [128, 4, 128], F32)) for i in range(2)]
            pkvs = [sbk.enter_context(nc.psum_tensor("pkvB%d" % i, [128, 512], F32)) for i in range(2)]

            S.dma("sp", lbc[:], lbc_in[:, :], writes=["lbc"])
            S.op("dve", lambda e: e.tensor_tensor(out=lbcd[:], in0=lbc[:, 0:8], in1=lbc[:, 8:16], op=ALU.subtract),
                 reads=["lbc"], writes=["lbcd"])
            S.op("act", lambda e: e.activation(out=lbcd[:], in_=lbcd[:], func=AF.Tanh, scale=0.5), reads=["lbcd"], writes=["lbcd"])
            S.op("dve", lambda e: e.tensor_scalar(out=ohc[:], in0=lbcd[:], scalar1=-0.25, scalar2=0.25, op0=ALU.mult, op1=ALU.add),
                 reads=["lbcd"], writes=["ohc"])
            S.op("dve", lambda e: e.tensor_scalar(out=nohc[:], in0=lbcd[:], scalar1=0.25, scalar2=-0.25, op0=ALU.mult, op1=ALU.add),
                 reads=["lbcd"], writes=["nohc"])
            S.dma("sp", gonh[:], gon_in[:, :], writes=["gonh"])
            S.op("dve", lambda e: e.tensor_scalar(out=gonh[:], in0=gonh[:], scalar1=0.5, scalar2=None, op0=ALU.mult),
                 reads=["gonh"], writes=["gonh"])
            S.dma("sp", maskrep[:].rearrange("p r h t -> p (r h t)"), maskrep_in.rearrange("p r h t -> p (r h t)"), writes=["maskrep"])

            def fm_proj(hT, hTkey, col0, wk="Whg"):
                pb, key = pb_rot.next()
                S.pe([lambda e, kc=kc: e.matmul(pb[:], Whg[:, kc, col0:col0 + 128], hT[:, kc, :], start=(kc == 0), stop=(kc == 7))
                      for kc in range(8)], reads=[hTkey, wk], writes=[key])
                return pb, key

            for seq in range(2):
                for dr in range(2):
                    S.op("dve", lambda e: e.tensor_copy(out=St[:], in_=Sinit[:, seq * 2 + dr, :]), reads=["Sinit", "St"], writes=["St"])
                    S.op("act", lambda e: e.activation(out=Sbfs[sctr[0] % 2][:], in_=St[:], func=AF.Copy), reads=["St", ("Sbf", sctr[0] % 2)],
                         writes=[("Sbf", sctr[0] % 2)])
                    gorder = list(range(8)) if dr == 0 else list(range(7, -1, -1))
                    torder = list(range(4)) if dr == 0 else list(range(3, -1, -1))
                    corder = [0, 1] if dr == 0 else [1, 0]
                    tiles = [(gi_, gq, t) for gi_, gq in enumerate(gorder) for t in torder]
                    ntl = len(tiles)

                    def ginfo(gi_, gq):
                        tok0 = seq * 4096 + gq * 512
                        bidx = 0
                        return tok0, bidx

                    def group_load(gi_, gq):
                        tok0, bidx = ginfo(gi_, gq)
                        hT = hTl[bidx]; hTkey = ("hTl", bidx)
                        S.dma("sp", hT[:], HT_scr[:, :, tok0:tok0 + 512], reads=["HT_scr"], writes=[hTkey])

                    def group_front(gi_, gq):
                        tok0, bidx = ginfo(gi_, gq)
                        hT = hTl[bidx]; hTkey = ("hTl", bidx)
                        for h in range(4):
                            pb, key = fm_proj(hT, hTkey, h * 128)
                            S.op("act", lambda e, pb=pb: e.activation(out=thq[:], in_=pb[:], func=AF.Tanh, scale=0.5),
                                 reads=[key], writes=["thq"])
                            S.op("dve", lambda e, pb=pb, h=h: e.scalar_tensor_tensor(out=qs[:, h, :], in0=thq[:], scalar=1.0, in1=pb[:],
                                                                                op0=ALU.add, op1=ALU.mult),
                                 reads=["thq", key], writes=["qs"])
                        for h in range(4):
                            pb, key = fm_proj(hT, hTkey, 1024 + dr * 512 + h * 128)
                            S.op("act", lambda e, pb=pb: e.activation(out=thq[:], in_=pb[:], func=AF.Tanh, scale=0.5),
                                 reads=[key], writes=["thq"])
                            S.op("dve", lambda e, h=h: e.tensor_scalar(out=kTt[:, h, :], in0=thq[:], scalar1=nohc[:, dr * 4 + h:dr * 4 + h + 1],
                                                                       scalar2=ohc[:, dr * 4 + h:dr * 4 + h + 1], op0=ALU.mult, op1=ALU.add),
                                 reads=["thq", "ohc", "nohc"], writes=["kTt"])
                        if dr == 1:
                            gp = gi_ % 2
                            for h in range(4):
                                pb, key = fm_proj(hT, hTkey, 2048 + h * 128)
                                S.op("act", lambda e, pb=pb: e.activation(out=thq[:], in_=pb[:], func=AF.Tanh, scale=0.5),
                                     reads=[key], writes=["thq"])
                                S.op("dve", lambda e, pb=pb: e.scalar_tensor_tensor(out=thq[:], in0=thq[:], scalar=1.0, in1=pb[:],
                                                                                  op0=ALU.add, op1=ALU.mult),
                                     reads=["thq", key], writes=["thq"])
                                S.op("dve", lambda e, h=h: e.tensor_scalar(out=sg2s[gp][:, h, :], in0=thq[:], scalar1=gonh[:, h:h + 1],
                                                                           scalar2=None, op0=ALU.mult),
                                     reads=["thq", "gonh"], writes=[("sg2", gp)])
                            S.dma("sp", ofwts[gp][:], OFW_scr[:, :, tok0:tok0 + 512], reads=["OFW_scr"], writes=[("ofwt", gp)])

                    def s1a_ln(i):
                        gl = gls[i % 2]
                        S.op("act", lambda e: e.activation(out=gl[:], in_=ff[:], func=AF.Ln), reads=["ff"], writes=[("gl", i % 2)])

                    def s1a(i):
                        gi_, gq, t = tiles[i]
                        tok0, bidx = ginfo(gi_, gq)
                        hT = hTl[bidx]; hTkey = ("hTl", bidx)
                        vb = vbs[i % 4]; kk = kks[i % 2]; gl = gls[i % 2]
                        tc = slice(t * 128, (t + 1) * 128)
                        pi, ikey = pb_rot.next()
                        S.pe([lambda e, kc=kc: e.matmul(pi[:], hT[:, kc, tc], Whg[:, kc, 512:1024],
                                                        start=(kc == 0), stop=(kc == 7)) for kc in range(8)],
                             reads=[hTkey, "Whg"], writes=[ikey])
                        S.op("act", lambda e: e.activation(out=vb[:], in_=pi[:], func=AF.Copy), reads=[ikey], writes=[("vb", i % 4)])
                        pf, fkey = pb_rot.next()
                        S.pe([lambda e, kc=kc: e.matmul(pf[:], hT[:, kc, tc], Whg[:, kc, 1024 + dr * 512:1536 + dr * 512],
                                                        start=(kc == 0), stop=(kc == 7)) for kc in range(8)],
                             reads=[hTkey, "Whg"], writes=[fkey])
                        S.op("act", lambda e: e.activation(out=th[:], in_=pf[:], func=AF.Tanh, scale=0.5),
                             reads=[fkey], writes=["th"])
                        S.op("dve", lambda e: e.tensor_tensor(out=Aa[:], in0=th[:], in1=oh_bc[:, dr, :], op=ALU.mult),
                             reads=["th", "oh_bc"], writes=["Aa"])
                        S.op("dve", lambda e: e.tensor_tensor(out=ff[:], in0=Aa[:], in1=c_bc[:, dr, :], op=ALU.add),
                             reads=["Aa", "c_bc"], writes=["ff"])
                        S.op("dve", lambda e: e.tensor_tensor(out=kk[:], in0=oh_bc[:, dr, :], in1=Aa[:], op=ALU.subtract),
                             reads=["Aa", "oh_bc"], writes=[("kk", i % 2)])

                    def s1b(i):
                        gi_, gq, t = tiles[i]
                        kk = kks[i % 2]; gl = gls[i % 2]
                        ks = kss[i % 3]; Ei = Eis[i % 3]; Qt = Qts[i % 3]; Kt = Kts[i % 2]
                        tc = slice(t * 128, (t + 1) * 128)
                        pe2, ekey = pb_rot.next()
                        S.pe([lambda e: e.matmul(pe2[:], cm[:, 2 + dr, :], gl[:], start=True, stop=True)],
                             reads=[("gl", i % 2), "cm"], writes=[ekey])
                        pbT, tkey = pb_rot.next()
                        pbT4 = pbT[:].rearrange("p (h t) -> p h t", h=4)
                        S.pe([lambda e, h=h: e.matmul(pbT4[:, h, :], gl[:, h * 128:(h + 1) * 128], cm[:, dr, :], start=True, stop=True)
                              for h in range(4)], reads=[("gl", i % 2), "cm"], writes=[tkey])
                        S.op("act", lambda e: e.activation(out=E2[:], in_=pe2[:], func=AF.Exp), reads=[ekey], writes=["E2"])
                        S.op("act", lambda e: e.activation(out=Ei[:], in_=pbT4, func=AF.Exp), reads=[tkey], writes=[("Ei", i % 3)])
                        S.op("act", lambda e: e.activation(out=Eni[:], in_=pbT4, func=AF.Exp, scale=-1.0), reads=[tkey], writes=["Eni"])
                        S.op("dve", lambda e: e.tensor_tensor(out=ks[:], in0=kk[:], in1=E2[:], op=ALU.mult),
                             reads=[("kk", i % 2), "E2"], writes=[("ks", i % 3)])
                        S.op("dve", lambda e: e.scalar_tensor_tensor(out=Qt[:], in0=qs[:, :, tc], scalar=0.5, in1=Ei[:],
                                                                     op0=ALU.mult, op1=ALU.mult), reads=["qs", ("Ei", i % 3)], writes=[("Qt", i % 3)])
                        S.op("dve", lambda e: e.tensor_tensor(out=Kt[:], in0=kTt[:, :, tc], in1=Eni[:], op=ALU.mult),
                             reads=["kTt", "Eni"], writes=[("Kt", i % 2)])

                    def s1c(i):
                        Qt = Qts[i % 3]; Kt = Kts[i % 2]; attm = attms[i % 2]
                        patt, akey = pb_rot.next()
                        patt4 = patt[:].rearrange("p (h t) -> p h t", h=4)
                        S.pe([lambda e, h=h: e.matmul(patt4[:, h, :], Kt[:, h, :], Qt[:, h, :], start=True, stop=True)
                              for h in range(4)], reads=[("Kt", i % 2), ("Qt", i % 3)], writes=[akey])
                        S.op("dve", lambda e: e.tensor_tensor(out=attm[:], in0=patt4, in1=maskrep[:, dr, :, :], op=ALU.mult),
                             reads=[akey, "maskrep"], writes=[("attm", i % 2)])

                    def s2(i, part):
                        vb = vbs[i % 4]; ks = kss[i % 3]; Ei = Eis[i % 3]; Qt = Qts[i % 3]; attm = attms[i % 2]
                        p = i % 2
                        po = pos_[p]
                        if part == 0:
                            S.pe([lambda e, h=h: e.matmul(po[:, h, :], vb[:, h * 128:(h + 1) * 128], attm[:, h, :], start=(h == 0), stop=False,
                                                          skip_group_check=True) for h in range(4)],
                                 reads=[("vb", i % 4), ("attm", i % 2)], writes=[("poB", p)])
                            for ci, c in enumerate(corder):
                                cs = slice(c * 64, (c + 1) * 64)
                                pkv = pkvs[ci]
                                S.pe([lambda e, h=h: e.matmul(pkv[:, h * 128:(h + 1) * 128], ks[cs, h * 128:(h + 1) * 128],
                                                              vb[cs, h * 128:(h + 1) * 128], start=True, stop=True) for h in range(4)],
                                     reads=[("ks", i % 3), ("vb", i % 4)], writes=[("pkvB", ci)])
                        cis = [0] if part == 0 else [1]
                        for ci in cis:
                            c = corder[ci]
                            cs = slice(c * 64, (c + 1) * 64)
                            csel = (c * 64 + 63) if dr == 0 else (c * 64)
                            pkv = pkvs[ci]
                            kq = sctr[0]
                            Sr = Sbfs[kq % 2]; Sw = Sbfs[(kq + 1) % 2]
                            S.pe([lambda e, h=h: e.matmul(po[:, h, cs], Sr[:, h * 128:(h + 1) * 128], Qt[:, h, cs], start=False,
                                                          stop=(ci == 1), skip_group_check=True) for h in range(4)],
                                 reads=[("Sbf", kq % 2), ("Qt", i % 3), ("poB", p)], writes=[("poB", p)])
                            St4 = St[:].rearrange("p (h e) -> p h e", h=4)
                            S.op("dve", lambda e: e.tensor_tensor(out=St4, in0=St4,
                                                                  in1=Ei[:, :, csel:csel + 1].broadcast_to([128, 4, 128]), op=ALU.mult),
                                 reads=["St", ("Ei", i % 3)], writes=["St"])
                            S.op("dve", lambda e: e.tensor_tensor(out=St[:], in0=St[:], in1=pkv[:], op=ALU.add),
                                 reads=["St", ("pkvB", ci)], writes=["St"])
                            S.op("act", lambda e: e.activation(out=Sw[:], in_=St[:], func=AF.Copy),
                                 reads=["St", ("Sbf", (kq + 1) % 2)], writes=[("Sbf", (kq + 1) % 2)])
                            sctr[0] += 1

                    def stage3(i):
                        gi_, gq, t = tiles[i]
                        tok0, bidx = ginfo(gi_, gq)
                        p = i % 2
                        po = pos_[p]
                        gp = gi_ % 2
                        tc = slice(t * 128, (t + 1) * 128)
                        if dr == 0:
                            S.op("act", lambda e: e.activation(out=ofwg[:, :, tc], in_=po[:], func=AF.Copy),
                                 reads=[("poB", p)], writes=["ofwg"])
                        else:
                            S.op("dve", lambda e: e.tensor_tensor(out=osum[:], in0=po[:], in1=ofwts[gp][:, :, tc], op=ALU.add),
                                 reads=[("poB", p), ("ofwt", gp)], writes=["osum"])
                            S.op("act", lambda e: e.activation(out=sqo[:], in_=osum[:], func=AF.Square),
                                 reads=["osum"], writes=["sqo"])
                            pss_, skey = pb_rot.next()
                            pss4 = pss_[:].rearrange("p (h t) -> p h t", h=4)
                            S.pe([lambda e, h=h: e.matmul(pss4[:, h, :], ones_bf[:], sqo[:, h, :], start=True, stop=True)
                                  for h in range(4)], reads=["sqo", "ones_bf"], writes=[skey])
                            S.op("dve", lambda e: e.tensor_scalar(out=mso[:], in0=pss_[:], scalar1=1.0 / 128, scalar2=EPS,
                                                                  op0=ALU.mult, op1=ALU.add), reads=[skey], writes=["mso"])
                            S.op("act", lambda e: e.activation(out=mso[:], in_=mso[:], func=AF.Ln), reads=["mso"], writes=["mso"])
                            S.op("act", lambda e: e.activation(out=rro[:], in_=mso[:], func=AF.Exp, scale=-0.5), reads=["mso"], writes=["mso"])
                            S.op("dve", lambda e: e.tensor_tensor(out=o2[:], in0=osum[:],
                                                                  in1=rro[:].rearrange("p (h t) -> p h t", h=4), op=ALU.mult),
                                 reads=["osum", "mso"], writes=["o2"])
                            S.op("dve", lambda e: e.tensor_tensor(out=ohg[:, :, tc], in0=o2[:], in1=sg2s[gp][:, :, tc], op=ALU.mult),
                                 reads=["o2", ("sg2", gp)], writes=["ohg"])
                        if t == torder[-1]:
                            if dr == 0:
                                S.dma("pool", OFW_scr[:, :, tok0:tok0 + 512], ofwg[:], reads=["ofwg"], writes=["OFW_scr"])
                            else:
                                S.dma("pool", OH_scr[:, :, tok0:tok0 + 512], ohg[:], reads=["ohg"], writes=["OH_scr"])

                    for i in range(ntl + 4):
                        if 0 <= i - 3 < ntl:
                            s2(i - 3, 0)
                            s2(i - 3, 1)
                        if i < ntl:
                            gi_, gq, t = tiles[i]
                            if t == torder[0]:
                                group_load(gi_, gq)
                            s1a(i)
                        if 0 <= i - 1 < ntl:
                            s1b(i - 1)
                        if i < ntl:
                            if t == torder[0]:
                                group_front(gi_, gq)
                            s1a_ln(i)
                        if 0 <= i - 2 < ntl:
                            s1c(i - 2)
                        if 0 <= i - 4 < ntl:
                            stage3(i - 4)
            S.barrier()

        sAB.close()

        if stop_after >= 3:
          with ExitStack() as sc:
            sbc = lambda n, s, d=F32: sc.enter_context(nc.sbuf_tensor("C_" + n, list(s), d))
            khs = [sbc("kh%d" % i, [128, 16384], BF16) for i in range(2)]
            vhs = [sbc("vh%d" % i, [128, 128, 65], BF16) for i in range(2)]
            qhs = [sbc("qh%d" % i, [96, 4096], BF16) for i in range(2)]
            NPT = 4
            pTs = [sbc("pT%d" % i, [128, 2, 512], BF16) for i in range(NPT)]
            OT = [sbc("OT%d" % i, [65, 512]) for i in range(2)]
            omT = [sbc("omT%d" % i, [64, 512], BF16) for i in range(2)]
            sel = sbc("sel", [65, 64])
            S.op("pool", lambda e: e.memset(sel[:], 0.0), writes=["sel"])
            S.op("pool", lambda e: e.memset(sel[64:65, :], 1.0), reads=["sel"], writes=["sel"])
            NPS = 3
            pss = [sc.enter_context(nc.psum_tensor("ps%d" % i, [128, 2, 512], F32)) for i in range(NPS)]
            pos = [sc.enter_context(nc.psum_tensor("po%d" % i, [128, 512], F32)) for i in range(1)]
            pbc = sc.enter_context(nc.psum_tensor("pbc", [128, 512], F32))
            seqs = [(0, 64, 0), (64, 128, 4096)]
            units = [(h, sq_) for h in range(8) for sq_ in range(2)]
            if debug == 3:
                units = units[:4]

            def load_unit(ui):
                h, sq_ = units[ui]
                kt0, nkt, _ = seqs[sq_]
                b = ui % 2
                r0 = (h % 2) * 64
                S.dma("sp", khs[b][0:64, 0:nkt * 128], K_scr[r0:r0 + 64, h // 2, kt0 * 128:(kt0 + nkt) * 128],
                      reads=["K_scr"], writes=[("kh", b)])
                S.dma("sp", khs[b][64:96, 0:nkt * 128], KPE_scr[:, kt0 * 128:(kt0 + nkt) * 128],
                      reads=["KPE_scr"], writes=[("kh", b)])
                for c0 in range(0, nkt, 32):
                    S.dma("sp", vhs[b][:, c0:c0 + 32, :], V_scr[h, :, kt0 + c0:kt0 + c0 + 32, :],
                          reads=["V_scr"], writes=[("vh", b)])
                S.dma("sp", qhs[b][:, :], Q_scr[h, :, sq_ * 4096:(sq_ + 1) * 4096], reads=["Q_scr"], writes=[("qh", b)])

            blocks = []
            for ui, (h, sq_) in enumerate(units):
                kt0, nkt, tok0 = seqs[sq_]
                for qg in range(8):
                    for kt in range(0, nkt, 2):
                        blocks.append((ui, h, sq_, qg, kt, nkt))
            LAG = 2
            pending = []

            def emit_qk(bi):
                ui, h, sq_, qg, kt, nkt = blocks[bi]
                b = ui % 2
                ps = pss[bi % NPS]
                S.pe([lambda e, j=j: e.matmul(ps[:, j, :], khs[b][0:96, (kt + j) * 128:(kt + j + 1) * 128],
                                              qhs[b][0:96, qg * 512:(qg + 1) * 512], start=True, stop=True) for j in range(2)],
                     reads=[("kh", b), ("qh", b)], writes=[("ps", bi % NPS)])
                S.op("act", lambda e: e.activation(out=pTs[bi % NPT][:], in_=ps[:], func=AF.Exp),
                     reads=[("ps", bi % NPS)], writes=[("pT", bi % NPT)])

            def emit_pv(bi):
                ui, h, sq_, qg, kt, nkt = blocks[bi]
                b = ui % 2
                gi = (ui * 8 + qg)
                po = pos[0]
                S.pe([lambda e, j=j: e.matmul(po[0:65, :], vhs[b][:, kt + j, :], pTs[bi % NPT][:, j, :], start=(kt + j == 0),
                                              stop=(kt + j == nkt - 1)) for j in range(2)],
                     reads=[("vh", b), ("pT", bi % NPT)], writes=["po"])
                if kt + 2 == nkt:
                    o = OT[gi % 2]; om = omT[gi % 2]
                    tok0 = seqs[sq_][2] + qg * 512
                    S.op("act", lambda e: e.activation(out=o[:], in_=po[0:65, :], func=AF.Copy),
                         reads=["po"], writes=[("OT", gi % 2)])
                    S.op("dve", lambda e: e.reciprocal(out=o[64:65, :], in_=o[64:65, :]),
                         reads=[("OT", gi % 2)], writes=[("OT", gi % 2)])

                    def fin():
                        S.pe([lambda e: e.matmul(pbc[0:64, :], sel[:], o[:], start=True, stop=True)],
                             reads=[("OT", gi % 2), "sel"], writes=["pbc"])
                        S.op("dve", lambda e: e.tensor_tensor(out=om[:], in0=o[0:64, :], in1=pbc[0:64, :], op=ALU.mult),
                             reads=[("OT", gi % 2), "pbc"], writes=[("omT", gi % 2)])
                        S.dma("pool", OM_scr[h, :, tok0:tok0 + 512], om[:], reads=[("omT", gi % 2)], writes=["OM_scr"])
                    pending.append([fin, 4])

            load_unit(0)
            nb = len(blocks)
            for bi in range(nb + LAG):
                if bi < nb:
                    ui, h, sq_, qg, kt, nkt = blocks[bi]
                    if qg == 0 and kt == 2 * LAG and ui + 1 < len(units):
                        load_unit(ui + 1)
                    emit_qk(bi)
                if bi - LAG >= 0:
                    emit_pv(bi - LAG)
                for p_ in list(pending):
                    p_[1] -= 1
                    if p_[1] <= 0:
                        p_[0]()
                        pending.remove(p_)
            for p_ in pending:
                p_[0]()
            S.barrier()


        own_rows = [(8 + i) for i in range(8)] + [(40 + i) for i in range(8)]
        if stop_after >= 4:
          with ExitStack() as sd:
            sbd = lambda n, s_, d=F32: sd.enter_context(nc.sbuf_tensor("D_" + n, list(s_), d))
            Wg = sbd("Wg", [128, 8, 2048], BF16)
            Wb0 = sbd("Wb0", [128, 4, 1024], BF16)
            Wb1 = sbd("Wb1", [64, 8, 1024], BF16)
            Wo = sbd("Wo", [128, 8, 1024], BF16)
            load_w(Wg, w_gates, 8, 2048, "Wg")
            load_w(Wb0, w_branch[0], 4, 1024, "Wb0")
            load_w(Wb1, w_branch[1], 8, 1024, "Wb1", pp=64)
            load_w(Wo, w_out, 8, 1024, "Wo")
            hTd = [sbd("hTd%d" % i, [128, 8, 512], BF16) for i in range(2)]
            ohT = [sbd("ohT%d" % i, [128, 4, 512], BF16) for i in range(2)]
            omT = [sbd("omTd%d" % i, [64, 8, 512], BF16) for i in range(2)]
            xd = sbd("xd", [128, 4, D])
            tg = [sbd("tg%d" % i, [128, 512]) for i in range(2)]
            m1 = sbd("m1", [128, 512]); m2 = sbd("m2", [128, 512])
            mg = sbd("mg", [128, 8, 512], BF16)
            x1o = [sbd("x1o%d" % i, [128, D]) for i in range(2)]
            pbs = [sd.enter_context(nc.psum_tensor("pbD%d" % i, [128, 512], F32)) for i in range(4)]
            pb_rot = Rot(pbs, "pbD")
            px = [sd.enter_context(nc.psum_tensor("pxD%d" % i, [128, 2, 512], F32)) for i in range(2)]
            for gi in range(16):
                b = gi % 2
                tok0 = gi * 512
                S.dma("sp", hTd[b][:], HT_scr[:, :, tok0:tok0 + 512], reads=["HT_scr"], writes=[("hTd", b)])
                S.dma("sp", ohT[b][:], OH_scr[:, :, tok0:tok0 + 512], reads=["OH_scr"], writes=[("ohT", b)])
                S.dma("sp", omT[b][:], OM_scr[:, :, tok0:tok0 + 512].rearrange("h e t -> e h t"), reads=["OM_scr"], writes=[("omTd", b)])
                S.dma("sp", xd[:], x_v[own_rows[gi]], writes=["xd"])
                for j in range(8):
                    for n_ in range(2):
                        pb, key = pb_rot.next()
                        c0 = n_ * 1024 + j * 128
                        S.pe([lambda e, kc=kc, pb=pb, c0=c0: e.matmul(pb[:], Wg[:, kc, c0:c0 + 128], hTd[b][:, kc, :],
                                                                       start=(kc == 0), stop=(kc == 7)) for kc in range(8)],
                             reads=[("hTd", b), "Wg"], writes=[key])
                        S.op("act", lambda e, pb=pb, n_=n_: e.activation(out=tg[n_][:], in_=pb[:], func=AF.Tanh, scale=0.5),
                             reads=[key], writes=[("tg", n_)])
                    pbh, hkey = pb_rot.next()
                    S.pe([lambda e, h=h, pbh=pbh: e.matmul(pbh[:], Wb0[:, h, j * 128:(j + 1) * 128], ohT[b][:, h, :],
                                                           start=(h == 0), stop=(h == 3)) for h in range(4)],
                         reads=[("ohT", b), "Wb0"], writes=[hkey])
                    S.op("dve", lambda e, pbh=pbh: e.scalar_tensor_tensor(out=m1[:], in0=tg[0][:], scalar=1.0, in1=pbh[:],
                                                                          op0=ALU.add, op1=ALU.mult),
                         reads=[("tg", 0), hkey], writes=["m1"])
                    pbm, mkey = pb_rot.next()
                    S.pe([lambda e, h=h, pbm=pbm: e.matmul(pbm[:], Wb1[:, h, j * 128:(j + 1) * 128], omT[b][:, h, :],
                                                           start=(h == 0), stop=(h == 7)) for h in range(8)],
                         reads=[("omTd", b), "Wb1"], writes=[mkey])
                    S.op("dve", lambda e, pbm=pbm: e.scalar_tensor_tensor(out=m2[:], in0=tg[1][:], scalar=1.0, in1=pbm[:],
                                                                          op0=ALU.add, op1=ALU.mult),
                         reads=[("tg", 1), mkey], writes=["m2"])
                    S.op("pool", lambda e, j=j: e.tensor_tensor(out=mg[:, j, :], in0=m1[:], in1=m2[:], op=ALU.add),
                         reads=["m1", "m2"], writes=["mg"])
                for t in range(4):
                    pxt = px[t % 2]
                    tcs = slice(t * 128, (t + 1) * 128)
                    S.pe([lambda e, j=j, nb=nb, pxt=pxt, tcs=tcs: e.matmul(pxt[:, nb, :], mg[:, j, tcs], Wo[:, j, nb * 512:(nb + 1) * 512],
                                                                  start=(j == 0), stop=(j == 7)) for nb in range(2) for j in range(8)],
                         reads=["mg", "Wo"], writes=[("pxD", t % 2)])
                    xo = x1o[t % 2]
                    S.op("dve", lambda e, pxt=pxt, xo=xo, t=t: e.scalar_tensor_tensor(out=xo[:], in0=pxt[:].rearrange("p a b -> p (a b)"),
                                                                           scalar=0.5, in1=xd[:, t, :], op0=ALU.mult, op1=ALU.add),
                         reads=[("pxD", t % 2), "xd"], writes=[("x1o", t % 2)])
                    S.dma("pool", X1_scr[tok0 + t * 128:tok0 + (t + 1) * 128, :], xo[:], reads=[("x1o", t % 2)], writes=["X1_scr"])
            S.barrier()

        if stop_after >= 5:
          with ExitStack() as se:
            sbe = lambda n, s_, d=F32: se.enter_context(nc.sbuf_tensor("E_" + n, list(s_), d))
            Wgu = sbe("Wgu", [128, 8, 2 * DFF], BF16)
            Wd = sbe("Wd", [128, 22, 1024], BF16)
            load_w(Wgu, w_gu, 8, 2 * DFF, "Wgu")
            load_w(Wd, w_down, 22, 1024, "Wd")
            gffn_bc = sbe("gffn_bc", [128, D]); gfin_bc = sbe("gfin_bc", [128, D])
            S.dma("sp", gffn_bc[:], g_ffn.broadcast_to([128, D]), writes=["gffn_bc"])
            S.dma("sp", gfin_bc[:], g_final.broadcast_to([128, D]), writes=["gfin_bc"])
            x1t = [sbe("x1t%d" % i, [128, D]) for i in range(4)]
            hn2 = [sbe("hn2_%d" % i, [128, D], BF16) for i in range(2)]
            h2T = sbe("h2T", [128, 8, 512], BF16)
            actT = sbe("actT", [128, 22, 512], BF16)
            sgt = [sbe("sgt%d" % i, [128, 512]) for i in range(2)]
            ss2 = sbe("ss2", [128, 8]); ms2 = sbe("ms2", [128, 8]); rs2 = sbe("rs2", [128, 8])
            ptr2 = [se.enter_context(nc.psum_tensor("ptrE%d" % i, [128, 8, 128], BF16)) for i in range(2)]
            pbs = [se.enter_context(nc.psum_tensor("pbE%d" % i, [128, 512], F32)) for i in range(4)]
            pb_rot = Rot(pbs, "pbE")
            px2 = se.enter_context(nc.psum_tensor("pxE", [128, 2, 512], F32))
            for gi in range(16):
                tok0 = gi * 512
                for t in range(4):
                    S.dma("sp", x1t[t][:], X1_scr[tok0 + t * 128:tok0 + (t + 1) * 128, :], reads=["X1_scr"], writes=[("x1t", t)])
                    S.op("act", lambda e, t=t: e.activation(out=hn2[t % 2][:], in_=x1t[t][:], func=AF.Square, accum_out=ss2[:, t:t + 1]),
                         reads=[("x1t", t)], writes=[("hn2", t % 2), ("ss2", t)])
                    S.op("dve", lambda e, t=t: e.tensor_scalar(out=ms2[:, t:t + 1], in0=ss2[:, t:t + 1], scalar1=1.0 / D, scalar2=EPS,
                                                               op0=ALU.mult, op1=ALU.add), reads=[("ss2", t)], writes=[("ms2", t)])
                    S.op("pool", lambda e, t=t: e.tensor_tensor(out=rs2[:, t:t + 1], in0=ms2[:, t:t + 1], in1=negh[:, 0:1], op=ALU.pow),
                         reads=[("ms2", t), "negh"], writes=[("rs2", t)])
                    hb = hn2[t % 2]
                    S.op("dve", lambda e, t=t, hb=hb: e.scalar_tensor_tensor(out=hb[:], in0=x1t[t][:], scalar=rs2[:, t:t + 1],
                                                                          in1=gffn_bc[:], op0=ALU.mult, op1=ALU.mult),
                         reads=[("x1t", t), ("rs2", t), "gffn_bc"], writes=[("hn2", t % 2)])
                    ptr = ptr2[t % 2]
                    S.pe([lambda e, kc=kc, ptr=ptr, hb=hb: e.transpose(ptr[:, kc, :], hb[:, kc * 128:(kc + 1) * 128], ident[:])
                          for kc in range(8)], reads=[("hn2", t % 2), "ident"], writes=[("ptrE", t % 2)])
                    S.op("act", lambda e, t=t, ptr=ptr: e.activation(out=h2T[:, :, t * 128:(t + 1) * 128], in_=ptr[:], func=AF.Copy),
                         reads=[("ptrE", t % 2)], writes=["h2T"])
                for fb in range(22):
                    pgt, gkey = pb_rot.next()
                    S.pe([lambda e, kc=kc, pgt=pgt, fb=fb: e.matmul(pgt[:], Wgu[:, kc, fb * 128:(fb + 1) * 128], h2T[:, kc, :],
                                                             start=(kc == 0), stop=(kc == 7)) for kc in range(8)],
                         reads=["h2T", "Wgu"], writes=[gkey])
                    put, ukey = pb_rot.next()
                    S.pe([lambda e, kc=kc, put=put, fb=fb: e.matmul(put[:], Wgu[:, kc, DFF + fb * 128:DFF + (fb + 1) * 128], h2T[:, kc, :],
                                                             start=(kc == 0), stop=(kc == 7)) for kc in range(8)],
                         reads=["h2T", "Wgu"], writes=[ukey])
                    sg_ = sgt[fb % 2]
                    S.op("act", lambda e, pgt=pgt, sg_=sg_: e.activation(out=sg_[:], in_=pgt[:], func=AF.Silu),
                         reads=[gkey], writes=[("sgt", fb % 2)])
                    S.op("dve", lambda e, put=put, sg_=sg_, fb=fb: e.tensor_tensor(out=actT[:, fb, :], in0=sg_[:], in1=put[:], op=ALU.mult),
                         reads=[("sgt", fb % 2), ukey], writes=["actT"])
                for t in range(4):
                    tcs = slice(t * 128, (t + 1) * 128)
                    S.pe([lambda e, fb=fb, nb=nb, tcs=tcs: e.matmul(px2[:, nb, :], actT[:, fb, tcs], Wd[:, fb, nb * 512:(nb + 1) * 512],
                                                           start=(fb == 0), stop=(fb == 21)) for nb in range(2) for fb in range(22)],
                         reads=["actT", "Wd"], writes=["pxE"])
                    yb = x1t[t]
                    S.op("dve", lambda e, t=t: e.tensor_tensor(out=x1t[t][:], in0=px2[:].rearrange("p a b -> p (a b)"), in1=x1t[t][:], op=ALU.add),
                         reads=["pxE", ("x1t", t)], writes=[("x1t", t)])
                    S.op("act", lambda e, t=t: e.activation(out=hn2[t % 2][:], in_=x1t[t][:], func=AF.Square, accum_out=ss2[:, 4 + t:5 + t]),
                         reads=[("x1t", t)], writes=[("hn2", t % 2), ("ss2", 4 + t)])
                    S.op("dve", lambda e, t=t: e.tensor_scalar(out=ms2[:, 4 + t:5 + t], in0=ss2[:, 4 + t:5 + t], scalar1=1.0 / D, scalar2=EPS,
                                                               op0=ALU.mult, op1=ALU.add), reads=[("ss2", 4 + t)], writes=[("ms2", 4 + t)])
                    S.op("pool", lambda e, t=t: e.tensor_tensor(out=rs2[:, 4 + t:5 + t], in0=ms2[:, 4 + t:5 + t], in1=negh[:, 0:1], op=ALU.pow),
                         reads=[("ms2", 4 + t), "negh"], writes=[("rs2", 4 + t)])
                    S.op("dve", lambda e, t=t, yb=yb: e.scalar_tensor_tensor(out=yb[:], in0=x1t[t][:], scalar=rs2[:, 4 + t:5 + t],
                                                                          in1=gfin_bc[:], op0=ALU.mult, op1=ALU.mult),
                         reads=[("x1t", t), ("rs2", 4 + t), "gfin_bc"], writes=[("x1t", t)])
                    S.dma("pool", y[tok0 + t * 128:tok0 + (t + 1) * 128, :], yb[:], reads=[("x1t", t)], writes=["y"])
            S.barrier()

        S.finish()
    return nc, S


SPLITS = np.cumsum([0, 512, 512, 512, 512, 512, 256, 256, 32, 2048])


def _rope_tables(pos, d=32, theta=10000.0):
    inv = theta ** (-np.arange(0, d, 2, dtype=np.float32) / d)
    ang = pos.astype(np.float32)[:, None] * inv[None, :].astype(np.float32)
    return np.cos(ang).astype(np.float32), np.sin(ang).astype(np.float32)


def make_consts():
    s = np.arange(128)[:, None]; t = np.arange(128)[None, :]
    same = (s // 64) == (t // 64)
    Mb_f = (same & (s <= t)).astype(np.float32)
    Mb_b = (same & (s >= t)).astype(np.float32)
    Mks_f = (same & (s > t)).astype(np.float32)
    Mks_b = (same & (s < t)).astype(np.float32)
    return np.stack([Mb_f, Mb_b, Mks_f, Mks_b]).astype(np.float32)


def prep_core(c, inp):
    f32 = np.float32
    p, a = c // 2, c % 2
    s, j = c // 4, c % 4
    xp = inp["x_prompt"]; xs = inp["x_sample"]
    w_in = inp["w_in"][0]
    col = lambda i: w_in[:, SPLITS[i]:SPLITS[i + 1]]
    slots = []
    oth = 1 - a
    pos = np.arange(oth * 4096, (oth + 1) * 4096)
    if a == 0:
        slots.append((xp[p, pos[::-1]], pos[::-1], 1))
    else:
        slots.append((xp[p, pos], pos, 0))
    chain = [(q, 0) for q in range(0, j)] + [(q, 1) for q in range(3, j, -1)]
    for q, d_ in chain:
        pos = np.arange(q * 4096, (q + 1) * 4096)
        if d_ == 1:
            pos = pos[::-1]
        slots.append((xs[s, pos], pos, d_))
    own_p_pos = np.arange(a * 4096, (a + 1) * 4096)
    own_s_pos = np.arange(j * 4096, (j + 1) * 4096)
    x_all = np.concatenate([slots[0][0], xp[p, own_p_pos], slots[1][0], slots[2][0], slots[3][0], xs[s, own_s_pos]], 0)
    pos_all = np.concatenate([slots[0][1], own_p_pos, slots[1][1], slots[2][1], slots[3][1], own_s_pos])
    cosk, sink = _rope_tables(pos_all)
    rope_k = np.stack([np.concatenate([cosk, cosk], 1).T, np.concatenate([sink, sink], 1).T]).astype(f32)
    pos_own = np.concatenate([own_p_pos, own_s_pos])
    cq, sq_ = _rope_tables(pos_own)
    scale = f32(96 ** -0.5)
    Cq = np.concatenate([np.ones((NOWN, 64), f32), cq, cq], 1).T * scale
    Sq = np.concatenate([np.zeros((NOWN, 64), f32), sq_, sq_], 1).T * scale
    rope_q = np.stack([Cq, Sq]).astype(f32)
    w_ctx = np.stack([np.concatenate([col(1), col(2 + d_)], 1) for (_, _, d_) in slots]).astype(f32)
    lbp_ctx = np.stack([inp["lb_param"][:, d_, :] for (_, _, d_) in slots]).astype(f32)
    flags = np.zeros((128, 16), f32)
    flags[:, 0] = 1.0 if a == 1 else 0.0
    flags[:, 1] = 1.0 if a == 0 else 0.0
    for k in (2, 3):
        flags[:, 2 + k] = 0.0 if (k - 1) == j else 1.0
    for k in (1, 2, 3):
        flags[:, 5 + k] = 1.0 if k == j else 0.0
    flags[:, 9] = 1.0 if j < 3 else 0.0
    w_ukv = inp["w_ukv"][0]
    d = {
        "x_all": x_all.astype(f32),
        "g_mix": inp["g_mix"].reshape(1, D),
        "w_kv": np.concatenate([col(6), col(7)], 1),
        "w_q": col(5),
        "w_ctx": w_ctx,
        "w_hg": np.concatenate([col(0), col(1), col(2), col(3), col(4)], 1),
        "w_gates": col(8),
        "w_uq": inp["w_uq"][0].reshape(256, 768),
        "w_uk": w_ukv[:, :, :64].reshape(256, 512),
        "w_uv": w_ukv[:, :, 64:].reshape(256, 512),
        "lbp_ctx": lbp_ctx,
        "lbp_own": inp["lb_param"].reshape(1, 2048),
        "g_onorm": inp["g_onorm"].reshape(1, 512),
        "g_qa": inp["g_qa"].reshape(2, 128).T,
        "g_kva": inp["g_kva"].reshape(2, 128).T,
        "w_branch": inp["w_branch"][0],
        "w_out": inp["w_out"][0],
        "g_ffn": inp["g_ffn"].reshape(1, D),
        "w_gu": inp["w_gate_up"][0],
        "w_down": inp["w_down"][0],
        "g_final": inp["g_final"].reshape(1, D),
        "cmat": make_consts(),
        "rope_q": rope_q,
        "rope_k": rope_k,
        "flags": flags,
        "lbc": inp["lb_param"].reshape(2, 2, 4, 128).transpose(3, 0, 1, 2).reshape(128, 16),
        "gon_c": inp["g_onorm"].reshape(4, 128).T,
        "maskrep": np.repeat(make_consts()[0:2].transpose(1, 0, 2)[:, :, None, :], 4, axis=2),
    }
    return {k: np.ascontiguousarray(v, dtype=np.float32) for k, v in d.items()}


_CACHE = {}


def kernel(**inputs):
    inp = {k: np.asarray(v) for k, v in inputs.items()}
    if "nc" not in _CACHE:
        _CACHE["nc"] = build_nc()[0]
    nc = _CACHE["nc"]
    in_maps = [prep_core(c, inp) for c in range(8)]
    res = run_bass_kernel_spmd(nc, in_maps, core_ids=list(range(8)))
    yp = np.zeros((4, 8192, D), np.float32)
    ys = np.zeros((2, 16384, D), np.float32)
    for c in range(8):
        yc = res.results[c]["y"]
        p, a = c // 2, c % 2
        s, j = c // 4, c % 4
        yp[p, a * 4096:(a + 1) * 4096] = yc[:4096]
        ys[s, j * 4096:(j + 1) * 4096] = yc[4096:]
    return (yp, ys)
```

```python
from contextlib import ExitStack
import numpy as np
import ml_dtypes
import concourse.bass as bass
import concourse.mybir as mybir
from concourse.bass_utils import run_bass_kernel_spmd

F32 = mybir.dt.float32
BF16 = mybir.dt.bfloat16
AF = mybir.ActivationFunctionType
ALU = mybir.AluOpType

D = 1024
NCTX = 16384
NOWN = 8192
NTOK = NCTX + NOWN
EPS = 1e-6
DFF = 2816


class Sched:
    def __init__(self, nc, stack, n_dma_sems=24):
        self.nc = nc
        self.eng = {"pe": nc.tensor, "act": nc.scalar, "dve": nc.vector, "pool": nc.gpsimd, "sp": nc.sync}
        self.sem = {}
        self.cnt = {}
        for k in ["pe", "act", "dve", "pool"]:
            self.sem[k] = stack.enter_context(nc.semaphore("s_" + k))
            self.cnt[k] = 0
        self.dma_sems = [stack.enter_context(nc.semaphore("s_dma%d" % i)) for i in range(n_dma_sems)]
        self.dma_cnt = [0] * n_dma_sems
        self.dma_rr = 0
        self.dma_rr_sw = 0
        self.seen = {k: {} for k in self.eng}
        self.lastw = {}
        self.readers = {}
        self.ninst = 0

    def _semobj(self, key):
        return self.sem[key] if isinstance(key, str) else self.dma_sems[key]

    def _wait(self, engname, dep):
        key, val = dep
        if key == engname and engname == "pe":
            return
        cur = self.seen[engname].get(key, 0)
        if cur >= val:
            return
        self.eng[engname].wait_ge(self._semobj(key), val)
        self.seen[engname][key] = val

    def _deps(self, engname, reads, writes):
        best = {}
        for r in reads:
            ev = self.lastw.get(r)
            if ev is not None and best.get(ev[0], 0) < ev[1]:
                best[ev[0]] = ev[1]
        for w in writes:
            ev = self.lastw.get(w)
            if ev is not None and best.get(ev[0], 0) < ev[1]:
                best[ev[0]] = ev[1]
            for ev in self.readers.get(w, ()):
                if best.get(ev[0], 0) < ev[1]:
                    best[ev[0]] = ev[1]
        for k, v in best.items():
            self._wait(engname, (k, v))

    def _record(self, ev, reads, writes):
        for r in reads:
            self.readers.setdefault(r, []).append(ev)
        for w in writes:
            self.lastw[w] = ev
            self.readers[w] = []

    def op(self, engname, fn, reads=(), writes=()):
        self._deps(engname, reads, writes)
        inst = fn(self.eng[engname])
        self.cnt[engname] += 1
        inst.then_inc(self.sem[engname], 1)
        ev = (engname, self.cnt[engname])
        self._record(ev, reads, writes)
        self.ninst += 1
        return ev

    def pe(self, fns, reads=(), writes=()):
        self._deps("pe", reads, writes)
        inst = None
        for fn in fns:
            inst = fn(self.eng["pe"])
            self.ninst += 1
        self.cnt["pe"] += 1
        inst.then_inc(self.sem["pe"], 1)
        ev = ("pe", self.cnt["pe"])
        self._record(ev, reads, writes)
        return ev

    def dma(self, qname, out, in_, reads=(), writes=(), **kw):
        half = len(self.dma_sems) // 2
        if qname == "pool":
            i = half + self.dma_rr_sw
            self.dma_rr_sw = (self.dma_rr_sw + 1) % (len(self.dma_sems) - half)
        else:
            i = self.dma_rr
            self.dma_rr = (self.dma_rr + 1) % half
        if self.dma_cnt[i] > 0:
            self._wait(qname, (i, self.dma_cnt[i]))
        self._deps(qname, reads, writes)
        self.dma_cnt[i] += 16
        self.eng[qname].dma_start(out=out, in_=in_, **kw).then_inc(self.dma_sems[i], 16)
        ev = (i, self.dma_cnt[i])
        self._record(ev, reads, writes)
        self.ninst += 1
        return ev

    def barrier(self):
        evs = [(k, self.cnt[k]) for k in ["pe", "act", "dve", "pool"] if self.cnt[k] > 0]
        evs += [(i, c) for i, c in enumerate(self.dma_cnt) if c > 0]
        for e in self.eng:
            for ev in evs:
                self._wait(e, ev)
        self.lastw = {}
        self.readers = {}

    def finish(self):
        for i, c in enumerate(self.dma_cnt):
            if c > 0:
                self._wait("sp", (i, c))


class Rot:
    def __init__(self, aps, name):
        self.aps = aps
        self.name = name
        self.i = 0

    def next(self):
        k = self.i % len(self.aps)
        self.i += 1
        return self.aps[k], (self.name, k)


def build_nc(debug=0, stop_after=99):
    nc = bass.Bass("TRN2", target_bir_lowering=False)
    dt_in = lambda n, s, d=F32: nc.dram_tensor(n, list(s), d, kind="ExternalInput").ap()
    x_all = dt_in("x_all", [NTOK, D])
    g_mix = dt_in("g_mix", [1, D])
    w_kv = dt_in("w_kv", [D, 288])
    w_q = dt_in("w_q", [D, 256])
    w_ctx = dt_in("w_ctx", [4, D, 1024])
    w_hg = dt_in("w_hg", [D, 2560])
    w_gates = dt_in("w_gates", [D, 2048])
    w_uq = dt_in("w_uq", [256, 768])
    w_uk = dt_in("w_uk", [256, 512])
    w_uv = dt_in("w_uv", [256, 512])
    lbp_ctx = dt_in("lbp_ctx", [4, 2, 512])
    lbp_own = dt_in("lbp_own", [1, 2048])
    g_onorm = dt_in("g_onorm", [1, 512])
    g_qa = dt_in("g_qa", [128, 2])
    g_kva = dt_in("g_kva", [128, 2])
    w_branch = dt_in("w_branch", [2, 512, D])
    w_out = dt_in("w_out", [D, D])
    g_ffn = dt_in("g_ffn", [1, D])
    w_gu = dt_in("w_gu", [D, 2 * DFF])
    w_down = dt_in("w_down", [DFF, D])
    g_final = dt_in("g_final", [1, D])
    cmat = dt_in("cmat", [4, 128, 128])
    rope_q = dt_in("rope_q", [2, 96, NOWN])
    rope_k = dt_in("rope_k", [2, 32, NTOK])
    flags = dt_in("flags", [128, 16])
    lbc_in = dt_in("lbc", [128, 16])
    gon_in = dt_in("gon_c", [128, 4])
    maskrep_in = dt_in("maskrep", [128, 2, 4, 128])
    y = nc.dram_tensor("y", [NOWN, D], F32, kind="ExternalOutput").ap()

    def scr(name, shape, dtype, dbg):
        kind = "ExternalOutput" if (debug and dbg) else "Internal"
        return nc.dram_tensor(name, list(shape), dtype, kind=kind).ap()

    K_scr = scr("K_scr", [128, 4, NTOK], BF16, debug == 1)
    KPE_scr = scr("KPE_scr", [32, NTOK], BF16, debug == 1)
    V_scr = scr("V_scr", [8, 128, NTOK // 128, 65], BF16, debug == 1)
    Q_scr = scr("Q_scr", [8, 96, NOWN], BF16, debug == 1)
    HT_scr = scr("HT_scr", [128, 8, NOWN], BF16, debug == 1)
    SI_scr = scr("SI_scr", [4, 128, 512], F32, debug == 1)
    OH_scr = scr("OH_scr", [128, 4, NOWN], BF16, debug == 2)
    OM_scr = scr("OM_scr", [8, 64, NOWN], BF16, debug == 3)
    X1_scr = scr("X1_scr", [NOWN, D], F32, debug == 4)
    OFW_scr = scr("OFW_scr", [128, 4, NOWN], F32, False)

    with ExitStack() as st:
        S = Sched(nc, st)
        sb = lambda n, s, d=F32: st.enter_context(nc.sbuf_tensor(n, list(s), d))

        ident_f = sb("ident_f", [128, 128])
        ident = sb("ident", [128, 128], BF16)
        ones_bf = sb("ones_bf", [128, 128], BF16)
        cm = sb("cm", [128, 4, 128])
        chunkind = sb("chunkind", [128, 2])
        flg = sb("flg", [128, 16])
        negh = sb("negh", [128, 512])
        gkva_c = sb("gkva_c", [128, 2])
        gqa_c = sb("gqa_c", [128, 2])
        sAB = ExitStack()
        sbab = lambda n, s_, d=F32: sAB.enter_context(nc.sbuf_tensor(n, list(s_), d))
        gmix_bc = sbab("gmix_bc", [128, D])
        Sinit = sbab("Sinit", [128, 4, 512])

        S.op("pool", lambda e: e.memset(ident_f[:], 0.0), writes=["ident_f"])
        S.op("pool", lambda e: e.affine_select(out=ident_f[:], in_=ident_f[:], pattern=[[-1, 128]],
                                                compare_op=ALU.not_equal, fill=1.0, base=0,
                                                channel_multiplier=1), reads=["ident_f"], writes=["ident_f"])
        S.op("pool", lambda e: e.tensor_copy(out=ident[:], in_=ident_f[:]), reads=["ident_f"], writes=["ident"])
        S.op("pool", lambda e: e.memset(ones_bf[:], 1.0), writes=["ones_bf"])
        S.op("pool", lambda e: e.memset(negh[:], -0.5), writes=["negh"])
        S.op("pool", lambda e: e.memset(chunkind[:], 0.0), writes=["chunkind"])
        S.op("pool", lambda e: e.memset(chunkind[0:64, 0:1], 1.0), reads=["chunkind"], writes=["chunkind"])
        S.op("pool", lambda e: e.memset(chunkind[64:128, 1:2], 1.0), reads=["chunkind"], writes=["chunkind"])
        S.op("pool", lambda e: e.memset(Sinit[:], 0.0), writes=["Sinit"])
        S.dma("sp", cm[:], cmat.rearrange("m p t -> p m t"), writes=["cm"])
        S.dma("sp", flg[:], flags[:, :], writes=["flg"])
        S.dma("sp", gmix_bc[:], g_mix.broadcast_to([128, D]), writes=["gmix_bc"])
        S.dma("sp", gkva_c[:], g_kva[:, :], writes=["gkva_c"])
        S.dma("sp", gqa_c[:], g_qa[:, :], writes=["gqa_c"])


        def load_w(dst, src, kc, cols, dst_key, col0=0, scale=None, pp=128):
            assert scale is None
            per = max(1, 4096 // cols)
            k0 = 0
            while k0 < kc:
                n = min(per, kc - k0)
                S.dma("pool", dst[0:pp, k0:k0 + n, col0:col0 + cols],
                      src[k0 * pp:(k0 + n) * pp, :].rearrange("(k p) c -> p k c", p=pp), writes=[dst_key])
                k0 += n

        Wkv = sbab("Wkv", [128, 8, 320], BF16)
        Wuk = sbab("Wuk", [128, 2, 512], BF16)
        Wuv = sbab("Wuv", [128, 2, 512], BF16)
        load_w(Wkv, w_kv, 8, 288, "Wkv")
        S.op("act", lambda e: e.activation(out=Wkv[:, :, 288:304], in_=Wkv[:, :, 272:288], func=AF.Copy, scale=-1.0),
             reads=["Wkv"], writes=["Wkv"])
        S.op("act", lambda e: e.activation(out=Wkv[:, :, 304:320], in_=Wkv[:, :, 256:272], func=AF.Copy),
             reads=["Wkv"], writes=["Wkv"])
        load_w(Wuk, w_uk, 2, 512, "Wuk")
        load_w(Wuv, w_uv, 2, 512, "Wuv")

        def rmsnorm_T(g, xt, hn, hT, hTkey, ptr_rot, ss, ms, rstd, junk):
            for t in range(4):
                S.op("act", lambda e, t=t: e.activation(out=junk[:], in_=xt[:, t, :], func=AF.Square,
                                                         accum_out=ss[:, t:t + 1]),
                     reads=["xt%d" % (g % 2)], writes=["junk", "ss"])
            S.op("dve", lambda e: e.tensor_scalar(out=ms[:], in0=ss[:], scalar1=1.0 / D, scalar2=EPS,
                                                  op0=ALU.mult, op1=ALU.add), reads=["ss"], writes=["ms"])
            S.op("pool", lambda e: e.tensor_tensor(out=rstd[:], in0=ms[:], in1=negh[:, 0:4], op=ALU.pow),
                 reads=["ms", "negh"], writes=["rstd"])
            for t in range(4):
                S.op("dve", lambda e, t=t: e.scalar_tensor_tensor(out=hn[:, t, :], in0=xt[:, t, :],
                                                                  scalar=rstd[:, t:t + 1], in1=gmix_bc[:],
                                                                  op0=ALU.mult, op1=ALU.mult),
                     reads=["xt%d" % (g % 2), "rstd", "gmix_bc"], writes=[("hn", t)])
            for t in range(4):
                ptr, pkey = ptr_rot.next()
                S.pe([lambda e, t=t, kc=kc, ptr=ptr: e.transpose(ptr[:, kc, :], hn[:, t, kc * 128:(kc + 1) * 128], ident[:])
                      for kc in range(8)], reads=[("hn", t), "ident"], writes=[pkey])
                S.op("act", lambda e, t=t, ptr=ptr: e.activation(out=hT[:, :, t * 128:(t + 1) * 128], in_=ptr[:],
                                                                  func=AF.Copy),
                     reads=[pkey], writes=[hTkey])

        x_v = x_all.rearrange("(g t p) d -> g p t d", t=4, p=128)

        ctx_groups = list(range(0, 8)) + list(range(16, 40))
        own_groups = list(range(8, 16)) + list(range(40, 48))

        with ExitStack() as sa:
            sba = lambda n, s, d=F32: sa.enter_context(nc.sbuf_tensor(n, list(s), d))
            xts = [sba("xt%d" % i, [128, 4, D]) for i in range(2)]
            hn = sba("hn", [128, 4, D], BF16)
            hTs = [sba("hT%d" % i, [128, 8, 512], BF16) for i in range(2)]
            junk = sba("junk", [128, D], BF16)
            ss = sba("ss", [128, 4]); ms = sba("ms", [128, 4]); rstd = sba("rstd", [128, 4])
            sq = sba("sq", [128, 2, 512], BF16)
            msk = sba("msk", [128, 512]); rk = sba("rk", [128, 512])
            ckvT = sba("ckvT", [128, 2, 512], BF16)
            rkt = sba("rkt", [32, 2, 512])
            t1 = sba("t1", [128, 512]); t2 = sba("t2", [128, 512])
            kpe = sba("kpe", [32, 512], BF16)
            kT = sba("kT", [128, 4, 512], BF16)
            vt = sba("vt", [128, 8, 4, 65], BF16)
            S.op("pool", lambda e: e.memset(vt[:], 1.0), writes=["vt"])
            ptrs = [sa.enter_context(nc.psum_tensor("ptr%d" % i, [128, 8, 128], BF16)) for i in range(2)]
            pbs = [sa.enter_context(nc.psum_tensor("pb%d" % i, [128, 512], F32)) for i in range(5)]
            ptr_rot = Rot(ptrs, "ptr")
            pb_rot = Rot(pbs, "pb")

            def kv_path(g, hT, hTkey):
                pk = []
                for blk in range(2):
                    pb, key = pb_rot.next()
                    S.pe([lambda e, kc=kc, blk=blk, pb=pb: e.matmul(pb[:], Wkv[:, kc, blk * 128:(blk + 1) * 128], hT[:, kc, :],
                                                                    start=(kc == 0), stop=(kc == 7)) for kc in range(8)],
                         reads=[hTkey, "Wkv"], writes=[key])
                    S.op("act", lambda e, blk=blk, pb=pb: e.activation(out=sq[:, blk, :], in_=pb[:], func=AF.Square),
                         reads=[key], writes=[("sq", blk)])
                    pk.append((pb, key))
                pss, skey = pb_rot.next()
                S.pe([lambda e, blk=blk: e.matmul(pss[:], ones_bf[:], sq[:, blk, :], start=(blk == 0), stop=(blk == 1))
                      for blk in range(2)], reads=[("sq", 0), ("sq", 1), "ones_bf"], writes=[skey])
                S.op("dve", lambda e: e.tensor_scalar(out=msk[:], in0=pss[:], scalar1=1.0 / 256, scalar2=EPS,
                                                      op0=ALU.mult, op1=ALU.add), reads=[skey], writes=["msk"])
                S.op("act", lambda e: e.activation(out=msk[:], in_=msk[:], func=AF.Ln), reads=["msk"], writes=["msk"])
                S.op("act", lambda e: e.activation(out=rk[:], in_=msk[:], func=AF.Exp, scale=-0.5), reads=["msk"], writes=["rk"])
                for blk in range(2):
                    pb, key = pk[blk]
                    S.op("dve", lambda e, blk=blk, pb=pb: e.scalar_tensor_tensor(
                        out=ckvT[:, blk, :], in0=pb[:], scalar=gkva_c[:, blk:blk + 1], in1=rk[:],
                        op0=ALU.mult, op1=ALU.mult), reads=[key, "rk", "gkva_c"], writes=["ckvT"])
                S.dma("sp", rkt[:], rope_k[:, :, g * 512:(g + 1) * 512].rearrange("c r t -> r c t"), writes=["rkt"])
                pkr, kkey = pb_rot.next()
                S.pe([lambda e, kc=kc: e.matmul(pkr[0:32, :], Wkv[:, kc, 256:288], hT[:, kc, :],
                                                start=(kc == 0), stop=(kc == 7)) for kc in range(8)],
                     reads=[hTkey, "Wkv"], writes=[kkey])
                pkrr, rkey = pb_rot.next()
                S.pe([lambda e, kc=kc: e.matmul(pkrr[0:32, :], Wkv[:, kc, 288:320], hT[:, kc, :],
                                                start=(kc == 0), stop=(kc == 7)) for kc in range(8)],
                     reads=[hTkey, "Wkv"], writes=[rkey])
                S.op("dve", lambda e: e.tensor_tensor(out=t1[0:32, :], in0=pkrr[0:32, :], in1=rkt[:, 1, :], op=ALU.mult),
                     reads=[rkey, "rkt"], writes=["t1"])
                S.op("dve", lambda e: e.tensor_tensor(out=t2[0:32, :], in0=pkr[0:32, :], in1=rkt[:, 0, :], op=ALU.mult),
                     reads=[kkey, "rkt"], writes=["t2"])
                S.op("dve", lambda e: e.tensor_tensor(out=kpe[:], in0=t1[0:32, :], in1=t2[0:32, :], op=ALU.add),
                     reads=["t1", "t2"], writes=["kpe"])
                S.dma("pool", KPE_scr[:, g * 512:(g + 1) * 512], kpe[:], reads=["kpe"], writes=["KPE_scr"])
                for pr in range(4):
                    pb, key = pb_rot.next()
                    S.pe([lambda e, kc=kc, pr=pr, pb=pb: e.matmul(pb[:], Wuk[:, kc, pr * 128:(pr + 1) * 128], ckvT[:, kc, :],
                                                                  start=(kc == 0), stop=(kc == 1)) for kc in range(2)],
                         reads=["ckvT", "Wuk"], writes=[key])
                    S.op("act", lambda e, pr=pr, pb=pb: e.activation(out=kT[:, pr, :], in_=pb[:], func=AF.Copy),
                         reads=[key], writes=["kT"])
                S.dma("pool", K_scr[:, :, g * 512:(g + 1) * 512], kT[:], reads=["kT"], writes=["K_scr"])
                for t in range(4):
                    pb, key = pb_rot.next()
                    S.pe([lambda e, kc=kc, t=t, pb=pb: e.matmul(pb[:], ckvT[:, kc, t * 128:(t + 1) * 128], Wuv[:, kc, :],
                                                                start=(kc == 0), stop=(kc == 1)) for kc in range(2)],
                         reads=["ckvT", "Wuv"], writes=[key])
                    S.op("act", lambda e, t=t, pb=pb: e.activation(out=vt[:, :, t, 0:64], in_=pb[:].rearrange("p (h e) -> p h e", e=64), func=AF.Copy),
                         reads=[key], writes=["vt"])
                S.dma("pool", V_scr[:, :, g * 4:(g + 1) * 4, :].rearrange("h p t e -> p h (t e)"),
                      vt[:].rearrange("p h t e -> p h (t e)"), reads=["vt"], writes=["V_scr"])

            with ExitStack() as s1:
                sb1 = lambda n, s, d=F32: s1.enter_context(nc.sbuf_tensor(n, list(s), d))
                Wctx = sb1("Wctx", [128, 8, 1024], BF16)
                lbt = sb1("lbt", [128, 2, 512]); lbd = sb1("lbd", [128, 512])
                oh_bc = sb1("oh_bc", [128, 512]); c_bc = sb1("c_bc", [128, 512])
                vb4 = [sb1("vb4_%d" % i, [128, 4, 512], BF16) for i in range(2)]
                ks4 = [sb1("ks4_%d" % i, [128, 4, 512], BF16) for i in range(2)]
                TA = sb1("TA", [128, 4, 512]); TC = sb1("TC", [128, 4, 512])
                TB = [sb1("TB%d" % i, [128, 4, 512]) for i in range(2)]
                dec4 = [sb1("dec4_%d" % i, [128, 32]) for i in range(2)]
                St = sb1("St", [128, 512])
                pdec = s1.enter_context(nc.psum_tensor("pdec", [128, 32], F32))
                S.op("pool", lambda e: e.memset(St[:], 0.0), writes=["St"])
                def gid(slot_, gi_):
                    return (slot_ * 8 + gi_) if slot_ == 0 else (16 + (slot_ - 1) * 8 + gi_)
                glist = [(slot_, gi_, gid(slot_, gi_)) for slot_ in range(4) for gi_ in range(8)]
                NG = len(glist)

                def front(n):
                    g = glist[n][2]
                    S.dma("sp", xts[g % 2][:], x_v[g], writes=["xt%d" % (g % 2)])
                    rmsnorm_T(g, xts[g % 2], hn, hTs[g % 2], "hT%d" % (g % 2), ptr_rot, ss, ms, rstd, junk)

                def kvp(n):
                    g = glist[n][2]
                    kv_path(g, hTs[g % 2], "hT%d" % (g % 2))

                def slot_start(slot):
                    load_w(Wctx, w_ctx[slot], 8, 1024, "Wctx")
                    S.dma("sp", lbt[:], lbp_ctx[slot:slot + 1].rearrange("o l f -> o (l f)").broadcast_to([128, 1024])
                          .rearrange("p (l f) -> p l f", l=2), writes=["lbt"])
                    S.op("dve", lambda e: e.tensor_tensor(out=lbd[:], in0=lbt[:, 0, :], in1=lbt[:, 1, :], op=ALU.subtract),
                         reads=["lbt"], writes=["lbd"])
                    S.op("act", lambda e: e.activation(out=lbd[:], in_=lbd[:], func=AF.Tanh, scale=0.5),
                         reads=["lbd"], writes=["lbd"])
                    S.op("dve", lambda e: e.tensor_scalar(out=oh_bc[:], in0=lbd[:], scalar1=-0.25, scalar2=0.25,
                                                          op0=ALU.mult, op1=ALU.add), reads=["lbd"], writes=["oh_bc"])
                    S.op("dve", lambda e: e.tensor_scalar(out=c_bc[:], in0=lbd[:], scalar1=0.25, scalar2=0.75,
                                                          op0=ALU.mult, op1=ALU.add), reads=["lbd"], writes=["c_bc"])
                    if slot >= 2:
                        S.op("dve", lambda e: e.tensor_scalar(out=St[:], in0=St[:], scalar1=flg[:, 2 + slot:3 + slot],
                                                              scalar2=None, op0=ALU.mult),
                             reads=["St", "flg"], writes=["St"])

                def slot_end(slot):
                    if slot == 0:
                        for d_ in range(2):
                            S.op("dve", lambda e, d_=d_: e.tensor_scalar(out=Sinit[:, d_, :], in0=St[:], scalar1=flg[:, d_:d_ + 1],
                                                                         scalar2=None, op0=ALU.mult),
                                 reads=["St", "flg"], writes=["Sinit"])
                        S.op("pool", lambda e: e.memset(St[:], 0.0), reads=["St"], writes=["St"])
                    else:
                        S.op("dve", lambda e: e.scalar_tensor_tensor(
                            out=Sinit[:, 2, :], in0=St[:], scalar=flg[:, 5 + slot:6 + slot], in1=Sinit[:, 2, :],
                            op0=ALU.mult, op1=ALU.add), reads=["St", "flg", "Sinit"], writes=["Sinit"])
                        if slot == 3:
                            S.op("dve", lambda e: e.tensor_scalar(out=Sinit[:, 3, :], in0=St[:], scalar1=flg[:, 9:10],
                                                                  scalar2=None, op0=ALU.mult),
                                 reads=["St", "flg"], writes=["Sinit"])

                oh4 = oh_bc[:].unsqueeze(1).broadcast_to([128, 4, 512])
                c4 = c_bc[:].unsqueeze(1).broadcast_to([128, 4, 512])

                def T1(n):
                    g = glist[n][2]
                    p = n % 2
                    hT = hTs[g % 2]; hTkey = "hT%d" % (g % 2)
                    for t in range(4):
                        pi, ikey = pb_rot.next()
                        S.pe([lambda e, kc=kc: e.matmul(pi[:], hT[:, kc, t * 128:(t + 1) * 128], Wctx[:, kc, 0:512],
                                                        start=(kc == 0), stop=(kc == 7)) for kc in range(8)],
                             reads=[hTkey, "Wctx"], writes=[ikey])
                        S.op("act", lambda e: e.activation(out=vb4[p][:, t, :], in_=pi[:], func=AF.Copy),
                             reads=[ikey], writes=[("vb4", p)])
                        pf, fkey = pb_rot.next()
                        S.pe([lambda e, kc=kc: e.matmul(pf[:], hT[:, kc, t * 128:(t + 1) * 128], Wctx[:, kc, 512:1024],
                                                        start=(kc == 0), stop=(kc == 7)) for kc in range(8)],
                             reads=[hTkey, "Wctx"], writes=[fkey])
                        S.op("act", lambda e: e.activation(out=TA[:, t, :], in_=pf[:], func=AF.Tanh, scale=0.5),
                             reads=[fkey], writes=["TA"])
                    S.op("dve", lambda e: e.tensor_tensor(out=TA[:], in0=TA[:], in1=oh4, op=ALU.mult),
                         reads=["TA", "oh_bc"], writes=["TA"])
                    S.op("dve", lambda e: e.tensor_tensor(out=TB[p][:], in0=TA[:], in1=c4, op=ALU.add),
                         reads=["TA", "c_bc"], writes=[("TB", p)])
                    S.op("dve", lambda e: e.tensor_tensor(out=TC[:], in0=oh4, in1=TA[:], op=ALU.subtract),
                         reads=["TA", "oh_bc"], writes=["TC"])

                def T1b(n):
                    p = n % 2
                    S.op("act", lambda e: e.activation(out=TB[p][:], in_=TB[p][:], func=AF.Ln),
                         reads=[("TB", p)], writes=[("TB", p)])

                def T2(n):
                    p = n % 2
                    gl4 = TB[p]
                    for t in range(4):
                        pe2, ekey = pb_rot.next()
                        S.pe([lambda e: e.matmul(pe2[:], cm[:, 2, :], gl4[:, t, :], start=True, stop=True)],
                             reads=[("TB", p), "cm"], writes=[ekey])
                        S.op("act", lambda e: e.activation(out=TA[:, t, :], in_=pe2[:], func=AF.Exp),
                             reads=[ekey], writes=["TA"])
                    S.pe([lambda e, t=t, h=h: e.matmul(pdec[:, t * 8 + 2 * h:t * 8 + 2 * h + 2], gl4[:, t, h * 128:(h + 1) * 128], chunkind[:],
                                                       start=True, stop=True) for t in range(4) for h in range(4)],
                         reads=[("TB", p), "chunkind"], writes=["pdec"])
                    S.op("act", lambda e: e.activation(out=dec4[p][:], in_=pdec[:], func=AF.Exp),
                         reads=["pdec"], writes=[("dec4", p)])
                    S.op("dve", lambda e: e.tensor_tensor(out=ks4[p][:], in0=TC[:], in1=TA[:], op=ALU.mult),
                         reads=["TC", "TA"], writes=[("ks4", p)])

                def U(n):
                    p = n % 2
                    for t in range(4):
                        for c in range(2):
                            pkv, vkey = pb_rot.next()
                            S.pe([lambda e, h=h: e.matmul(
                                pkv[:, h * 128:(h + 1) * 128], ks4[p][c * 64:(c + 1) * 64, t, h * 128:(h + 1) * 128],
                                vb4[p][c * 64:(c + 1) * 64, t, h * 128:(h + 1) * 128], start=True, stop=True) for h in range(4)],
                                reads=[("ks4", p), ("vb4", p)], writes=[vkey])
                            for h in range(4):
                                S.op("dve", lambda e, h=h: e.scalar_tensor_tensor(
                                    out=St[:, h * 128:(h + 1) * 128], in0=St[:, h * 128:(h + 1) * 128],
                                    scalar=dec4[p][:, t * 8 + 2 * h + c:t * 8 + 2 * h + c + 1], in1=pkv[:, h * 128:(h + 1) * 128],
                                    op0=ALU.mult, op1=ALU.add), reads=["St", ("dec4", p), vkey], writes=["St"])

                front(0); front(1); kvp(0)
                for n in range(NG):
                    slot, gi, g = glist[n]
                    if gi == 0:
                        slot_start(slot)
                    T1(n)
                    if n + 2 < NG:
                        front(n + 2)
                    T1b(n)
                    if n + 1 < NG:
                        kvp(n + 1)
                    T2(n)
                    U(n)
                    if gi == 7:
                        slot_end(slot)
                S.dma("pool", SI_scr.rearrange("s p f -> p s f"), Sinit[:], reads=["Sinit"], writes=["SI_scr"])
                S.barrier()

            with ExitStack() as s2:
                sb2 = lambda n, s, d=F32: s2.enter_context(nc.sbuf_tensor(n, list(s), d))
                Wq = sb2("Wq", [128, 8, 256], BF16)
                Wuq = sb2("Wuq", [128, 2, 768], BF16)
                Wuqr = sb2("Wuqr", [128, 2, 768], BF16)
                load_w(Wq, w_q, 8, 256, "Wq")
                load_w(Wuq, w_uq, 2, 768, "Wuq")
                S.op("pool", lambda e: e.memset(Wuqr[:], 0.0), writes=["Wuqr"])
                Wuq4 = Wuq[:].rearrange("p k (h c) -> p k h c", c=96)
                Wuqr4 = Wuqr[:].rearrange("p k (h c) -> p k h c", c=96)
                for kc in range(2):
                    S.op("act", lambda e, kc=kc: e.activation(out=Wuqr4[:, kc, :, 64:80], in_=Wuq4[:, kc, :, 80:96], func=AF.Copy,
                                                              scale=-1.0), reads=["Wuq", "Wuqr"], writes=["Wuqr"])
                    S.op("act", lambda e, kc=kc: e.activation(out=Wuqr4[:, kc, :, 80:96], in_=Wuq4[:, kc, :, 64:80], func=AF.Copy),
                         reads=["Wuq", "Wuqr"], writes=["Wuqr"])
                cqT = sb2("cqT", [128, 2, 512], BF16)
                rqt = sb2("rqt", [96, 2, 512])
                qT = sb2("qT", [96, 8, 512], BF16)
                def front2(g):
                    S.dma("sp", xts[g % 2][:], x_v[g], writes=["xt%d" % (g % 2)])
                    rmsnorm_T(g, xts[g % 2], hn, hTs[g % 2], "hT%d" % (g % 2), ptr_rot, ss, ms, rstd, junk)
                front2(own_groups[0])
                for oi, g in enumerate(own_groups):
                    xt = xts[g % 2]; hT = hTs[g % 2]; hTkey = "hT%d" % (g % 2)
                    if oi + 1 < len(own_groups):
                        front2(own_groups[oi + 1])
                    S.dma("pool", HT_scr[:, :, oi * 512:(oi + 1) * 512], hT[:], reads=[hTkey], writes=["HT_scr"])
                    kv_path(g, hT, hTkey)
                    pk = []
                    for blk in range(2):
                        pb, key = pb_rot.next()
                        S.pe([lambda e, kc=kc, blk=blk, pb=pb: e.matmul(pb[:], Wq[:, kc, blk * 128:(blk + 1) * 128], hT[:, kc, :],
                                                                        start=(kc == 0), stop=(kc == 7)) for kc in range(8)],
                             reads=[hTkey, "Wq"], writes=[key])
                        S.op("act", lambda e, blk=blk, pb=pb: e.activation(out=sq[:, blk, :], in_=pb[:], func=AF.Square),
                             reads=[key], writes=[("sq", blk)])
                        pk.append((pb, key))
                    pss, skey = pb_rot.next()
                    S.pe([lambda e, blk=blk: e.matmul(pss[:], ones_bf[:], sq[:, blk, :], start=(blk == 0), stop=(blk == 1))
                          for blk in range(2)], reads=[("sq", 0), ("sq", 1), "ones_bf"], writes=[skey])
                    S.op("dve", lambda e: e.tensor_scalar(out=msk[:], in0=pss[:], scalar1=1.0 / 256, scalar2=EPS,
                                                          op0=ALU.mult, op1=ALU.add), reads=[skey], writes=["msk"])
                    S.op("act", lambda e: e.activation(out=msk[:], in_=msk[:], func=AF.Ln), reads=["msk"], writes=["msk"])
                    S.op("act", lambda e: e.activation(out=rk[:], in_=msk[:], func=AF.Exp, scale=-0.5), reads=["msk"], writes=["rk"])
                    for blk in range(2):
                        pb, key = pk[blk]
                        S.op("dve", lambda e, blk=blk, pb=pb: e.scalar_tensor_tensor(
                            out=cqT[:, blk, :], in0=pb[:], scalar=gqa_c[:, blk:blk + 1], in1=rk[:],
                            op0=ALU.mult, op1=ALU.mult), reads=[key, "rk", "gqa_c"], writes=["cqT"])
                    S.dma("sp", rqt[:], rope_q[:, :, oi * 512:(oi + 1) * 512].rearrange("c r t -> r c t"), writes=["rqt"])
                    for h in range(8):
                        qa, akey = pb_rot.next()
                        S.pe([lambda e, kc=kc, h=h, qa=qa: e.matmul(qa[0:96, :], Wuq[:, kc, h * 96:(h + 1) * 96], cqT[:, kc, :],
                                                                    start=(kc == 0), stop=(kc == 1)) for kc in range(2)],
                             reads=["cqT", "Wuq"], writes=[akey])
                        qb, bkey = pb_rot.next()
                        S.pe([lambda e, kc=kc, h=h, qb=qb: e.matmul(qb[0:96, :], Wuqr[:, kc, h * 96:(h + 1) * 96], cqT[:, kc, :],
                                                                    start=(kc == 0), stop=(kc == 1)) for kc in range(2)],
                             reads=["cqT", "Wuqr"], writes=[bkey])
                        S.op("dve", lambda e, qb=qb: e.tensor_tensor(out=t1[0:96, :], in0=qb[0:96, :], in1=rqt[:, 1, :], op=ALU.mult),
                             reads=[bkey, "rqt"], writes=["t1"])
                        S.op("dve", lambda e, qa=qa: e.tensor_tensor(out=t2[0:96, :], in0=qa[0:96, :], in1=rqt[:, 0, :], op=ALU.mult),
                             reads=[akey, "rqt"], writes=["t2"])
                        S.op("dve", lambda e, h=h: e.tensor_tensor(out=qT[:, h, :], in0=t1[0:96, :], in1=t2[0:96, :], op=ALU.add),
                             reads=["t1", "t2"], writes=["qT"])
                    S.dma("pool", Q_scr[:, :, oi * 512:(oi + 1) * 512].rearrange("h r t -> r h t"), qT[:],
                          reads=["qT"], writes=["Q_scr"])
                S.barrier()

        if stop_after <= 1:
            sAB.close()
            S.finish()
            return nc, S


        if stop_after >= 2:
          with ExitStack() as sbk:
            sbb = lambda n, s, d=F32: sbk.enter_context(nc.sbuf_tensor("B_" + n, list(s), d))
            Whg = sbb("Whg", [128, 8, 2560], BF16)
            load_w(Whg, w_hg, 8, 2560, "Whg")
            hTl = [sbb("hTl0", [128, 8, 512], BF16)] * 2
            oh_bc = sbb("oh_bc", [128, 2, 512]); c_bc = sbb("c_bc", [128, 2, 512])
            with nc.sbuf_tensor("B_lbt", [128, 2, 1024], F32) as lbt, nc.sbuf_tensor("B_lbd", [128, 1024], F32) as lbd:
                S.dma("sp", lbt[:], lbp_own.broadcast_to([128, 2048]).rearrange("p (l f) -> p l f", l=2), writes=["lbt"])
                S.op("dve", lambda e: e.tensor_tensor(out=lbd[:], in0=lbt[:, 0, :], in1=lbt[:, 1, :], op=ALU.subtract),
                     reads=["lbt"], writes=["lbd"])
                S.op("act", lambda e: e.activation(out=lbd[:], in_=lbd[:], func=AF.Tanh, scale=0.5), reads=["lbd"], writes=["lbd"])
                S.op("dve", lambda e: e.tensor_scalar(out=oh_bc[:].rearrange("p r f -> p (r f)"), in0=lbd[:], scalar1=-0.25, scalar2=0.25,
                                                      op0=ALU.mult, op1=ALU.add), reads=["lbd"], writes=["oh_bc"])
                S.op("dve", lambda e: e.tensor_scalar(out=c_bc[:].rearrange("p r f -> p (r f)"), in0=lbd[:], scalar1=0.25, scalar2=0.75,
                                                      op0=ALU.mult, op1=ALU.add), reads=["lbd"], writes=["c_bc"])
                S.barrier()
            lbc = sbb("lbc", [128, 16]); lbcd = sbb("lbcd", [128, 8])
            ohc = sbb("ohc", [128, 8]); nohc = sbb("nohc", [128, 8])
            gonh = sbb("gonh", [128, 4])
            maskrep = sbb("maskrep", [128, 2, 4, 128])
            qs = sbb("qs", [128, 4, 512]); kTt = sbb("kTt", [128, 4, 512])
            sg2s = [sbb("sg2_%d" % i, [128, 4, 512]) for i in range(2)]
            thq = sbb("thq", [128, 512])
            vbs = [sbb("vb%d" % i, [128, 512], BF16) for i in range(4)]; th = sbb("th", [128, 512]); Aa = sbb("Aa", [128, 512])
            ff = sbb("ff", [128, 512]); kks = [sbb("kk%d" % i, [128, 512]) for i in range(2)]
            gls = [sbb("gl%d" % i, [128, 512]) for i in range(2)]
            E2 = sbb("E2", [128, 512]); kss = [sbb("ks%d" % i, [128, 512], BF16) for i in range(3)]
            Eis = [sbb("Ei%d" % i, [128, 4, 128]) for i in range(3)]; Eni = sbb("Eni", [128, 4, 128])
            Qts = [sbb("Qt%d" % i, [128, 4, 128], BF16) for i in range(3)]; Kts = [sbb("Kt%d" % i, [128, 4, 128], BF16) for i in range(2)]
            attms = [sbb("attm%d" % i, [128, 4, 128], BF16) for i in range(2)]
            St = sbb("St", [128, 512]); Sbfs = [sbb("Sbf%d" % i, [128, 512], BF16) for i in range(2)]
            sctr = [0]
            ofwg = sbb("ofwg", [128, 4, 512]); ofwts = [sbb("ofwt%d" % i, [128, 4, 512]) for i in range(2)]
            osum = sbb("osum", [128, 4, 128]); osgs = [sbb("osg%d" % i, [128, 4, 128]) for i in range(3)]
            sqo = sbb("sqo", [128, 4, 128], BF16); msos = [sbb("mso%d" % i, [128, 512]) for i in range(3)]
            ohg = sbb("ohg", [128, 4, 512], BF16)
            pbs = [sbk.enter_context(nc.psum_tensor("pbB%d" % i, [128, 512], F32)) for i in range(4)]
            pb_rot = Rot(pbs, "pbB")
            pos_ = [sbk.enter_context(nc.psum_tensor("poB%d" % i, [128, 4, 128], F32)) for i in range(2)]
            pkvs = [sbk.enter_context(nc.psum_tensor("pkvB%d" % i, [128, 512], F32)) for i in range(2)]

            S.dma("sp", lbc[:], lbc_in[:, :], writes=["lbc"])
            S.op("dve", lambda e: e.tensor_tensor(out=lbcd[:], in0=lbc[:, 0:8], in1=lbc[:, 8:16], op=ALU.subtract),
                 reads=["lbc"], writes=["lbcd"])
            S.op("act", lambda e: e.activation(out=lbcd[:], in_=lbcd[:], func=AF.Tanh, scale=0.5), reads=["lbcd"], writes=["lbcd"])
            S.op("dve", lambda e: e.tensor_scalar(out=ohc[:], in0=lbcd[:], scalar1=-0.25, scalar2=0.25, op0=ALU.mult, op1=ALU.add),
                 reads=["lbcd"], writes=["ohc"])
            S.op("dve", lambda e: e.tensor_scalar(out=nohc[:], in0=lbcd[:], scalar1=0.25, scalar2=-0.25, op0=ALU.mult, op1=ALU.add),
                 reads=["lbcd"], writes=["nohc"])
            S.dma("sp", gonh[:], gon_in[:, :], writes=["gonh"])
            S.op("dve", lambda e: e.tensor_scalar(out=gonh[:], in0=gonh[:], scalar1=0.5, scalar2=None, op0=ALU.mult),
                 reads=["gonh"], writes=["gonh"])
            S.dma("sp", maskrep[:].rearrange("p r h t -> p (r h t)"), maskrep_in.rearrange("p r h t -> p (r h t)"), writes=["maskrep"])

            def fm_proj(hT, hTkey, col0, wk="Whg"):
                pb, key = pb_rot.next()
                S.pe([lambda e, kc=kc: e.matmul(pb[:], Whg[:, kc, col0:col0 + 128], hT[:, kc, :], start=(kc == 0), stop=(kc == 7))
                      for kc in range(8)], reads=[hTkey, wk], writes=[key])
                return pb, key

            for seq in range(2):
                for dr in range(2):
                    S.op("dve", lambda e: e.tensor_copy(out=St[:], in_=Sinit[:, seq * 2 + dr, :]), reads=["Sinit", "St"], writes=["St"])
                    S.op("act", lambda e: e.activation(out=Sbfs[sctr[0] % 2][:], in_=St[:], func=AF.Copy), reads=["St", ("Sbf", sctr[0] % 2)],
                         writes=[("Sbf", sctr[0] % 2)])
                    gorder = list(range(8)) if dr == 0 else list(range(7, -1, -1))
                    torder = list(range(4)) if dr == 0 else list(range(3, -1, -1))
                    corder = [0, 1] if dr == 0 else [1, 0]
                    tiles = [(gi_, gq, t) for gi_, gq in enumerate(gorder) for t in torder]
                    ntl = len(tiles)

                    def ginfo(gi_, gq):
                        tok0 = seq * 4096 + gq * 512
                        bidx = 0
                        return tok0, bidx

                    def group_load(gi_, gq):
                        tok0, bidx = ginfo(gi_, gq)
                        hT = hTl[bidx]; hTkey = ("hTl", bidx)
                        S.dma("sp", hT[:], HT_scr[:, :, tok0:tok0 + 512], reads=["HT_scr"], writes=[hTkey])

                    def group_front(gi_, gq):
                        tok0, bidx = ginfo(gi_, gq)
                        hT = hTl[bidx]; hTkey = ("hTl", bidx)
                        for h in range(4):
                            pb, key = fm_proj(hT, hTkey, h * 128)
                            S.op("act", lambda e, pb=pb: e.activation(out=thq[:], in_=pb[:], func=AF.Tanh, scale=0.5),
                                 reads=[key], writes=["thq"])
                            S.op("dve", lambda e, pb=pb, h=h: e.scalar_tensor_tensor(out=qs[:, h, :], in0=thq[:], scalar=1.0, in1=pb[:],
                                                                                op0=ALU.add, op1=ALU.mult),
                                 reads=["thq", key], writes=["qs"])
                        for h in range(4):
                            pb, key = fm_proj(hT, hTkey, 1024 + dr * 512 + h * 128)
                            S.op("act", lambda e, pb=pb: e.activation(out=thq[:], in_=pb[:], func=AF.Tanh, scale=0.5),
                                 reads=[key], writes=["thq"])
                            S.op("dve", lambda e, h=h: e.tensor_scalar(out=kTt[:, h, :], in0=thq[:], scalar1=nohc[:, dr * 4 + h:dr * 4 + h + 1],
                                                                       scalar2=ohc[:, dr * 4 + h:dr * 4 + h + 1], op0=ALU.mult, op1=ALU.add),
                                 reads=["thq", "ohc", "nohc"], writes=["kTt"])
                        if dr == 1:
                            gp = gi_ % 2
                            for h in range(4):
                                pb, key = fm_proj(hT, hTkey, 2048 + h * 128)
                                S.op("act", lambda e, pb=pb: e.activation(out=thq[:], in_=pb[:], func=AF.Tanh, scale=0.5),
                                     reads=[key], writes=["thq"])
                                S.op("dve", lambda e, pb=pb: e.scalar_tensor_tensor(out=thq[:], in0=thq[:], scalar=1.0, in1=pb[:],
                                                                                  op0=ALU.add, op1=ALU.mult),
                                     reads=["thq", key], writes=["thq"])
                                S.op("dve", lambda e, h=h: e.tensor_scalar(out=sg2s[gp][:, h, :], in0=thq[:], scalar1=gonh[:, h:h + 1],
                                                                           scalar2=None, op0=ALU.mult),
                                     reads=["thq", "gonh"], writes=[("sg2", gp)])
                            S.dma("sp", ofwts[gp][:], OFW_scr[:, :, tok0:tok0 + 512], reads=["OFW_scr"], writes=[("ofwt", gp)])

                    def s1a_ln(i):
                        gl = gls[i % 2]
                        S.op("act", lambda e: e.activation(out=gl[:], in_=ff[:], func=AF.Ln), reads=["ff"], writes=[("gl", i % 2)])

                    def s1a(i):
                        gi_, gq, t = tiles[i]
                        tok0, bidx = ginfo(gi_, gq)
                        hT = hTl[bidx]; hTkey = ("hTl", bidx)
                        vb = vbs[i % 4]; kk = kks[i % 2]; gl = gls[i % 2]
                        tc = slice(t * 128, (t + 1) * 128)
                        pi, ikey = pb_rot.next()
                        S.pe([lambda e, kc=kc: e.matmul(pi[:], hT[:, kc, tc], Whg[:, kc, 512:1024],
                                                        start=(kc == 0), stop=(kc == 7)) for kc in range(8)],
                             reads=[hTkey, "Whg"], writes=[ikey])
                        S.op("act", lambda e: e.activation(out=vb[:], in_=pi[:], func=AF.Copy), reads=[ikey], writes=[("vb", i % 4)])
                        pf, fkey = pb_rot.next()
                        S.pe([lambda e, kc=kc: e.matmul(pf[:], hT[:, kc, tc], Whg[:, kc, 1024 + dr * 512:1536 + dr * 512],
                                                        start=(kc == 0), stop=(kc == 7)) for kc in range(8)],
                             reads=[hTkey, "Whg"], writes=[fkey])
                        S.op("act", lambda e: e.activation(out=th[:], in_=pf[:], func=AF.Tanh, scale=0.5),
                             reads=[fkey], writes=["th"])
                        S.op("dve", lambda e: e.tensor_tensor(out=Aa[:], in0=th[:], in1=oh_bc[:, dr, :], op=ALU.mult),
                             reads=["th", "oh_bc"], writes=["Aa"])
                        S.op("dve", lambda e: e.tensor_tensor(out=ff[:], in0=Aa[:], in1=c_bc[:, dr, :], op=ALU.add),
                             reads=["Aa", "c_bc"], writes=["ff"])
                        S.op("dve", lambda e: e.tensor_tensor(out=kk[:], in0=oh_bc[:, dr, :], in1=Aa[:], op=ALU.subtract),
                             reads=["Aa", "oh_bc"], writes=[("kk", i % 2)])

                    def s1b(i):
                        gi_, gq, t = tiles[i]
                        kk = kks[i % 2]; gl = gls[i % 2]
                        ks = kss[i % 3]; Ei = Eis[i % 3]; Qt = Qts[i % 3]; Kt = Kts[i % 2]
                        tc = slice(t * 128, (t + 1) * 128)
                        pe2, ekey = pb_rot.next()
                        S.pe([lambda e: e.matmul(pe2[:], cm[:, 2 + dr, :], gl[:], start=True, stop=True)],
                             reads=[("gl", i % 2), "cm"], writes=[ekey])
                        pbT, tkey = pb_rot.next()
                        pbT4 = pbT[:].rearrange("p (h t) -> p h t", h=4)
                        S.pe([lambda e, h=h: e.matmul(pbT4[:, h, :], gl[:, h * 128:(h + 1) * 128], cm[:, dr, :], start=True, stop=True)
                              for h in range(4)], reads=[("gl", i % 2), "cm"], writes=[tkey])
                        S.op("act", lambda e: e.activation(out=E2[:], in_=pe2[:], func=AF.Exp), reads=[ekey], writes=["E2"])
                        S.op("act", lambda e: e.activation(out=Ei[:], in_=pbT4, func=AF.Exp), reads=[tkey], writes=[("Ei", i % 3)])
                        S.op("act", lambda e: e.activation(out=Eni[:], in_=pbT4, func=AF.Exp, scale=-1.0), reads=[tkey], writes=["Eni"])
                        S.op("dve", lambda e: e.tensor_tensor(out=ks[:], in0=kk[:], in1=E2[:], op=ALU.mult),
                             reads=[("kk", i % 2), "E2"], writes=[("ks", i % 3)])
                        S.op("dve", lambda e: e.scalar_tensor_tensor(out=Qt[:], in0=qs[:, :, tc], scalar=0.5, in1=Ei[:],
                                                                     op0=ALU.mult, op1=ALU.mult), reads=["qs", ("Ei", i % 3)], writes=[("Qt", i % 3)])
                        S.op("dve", lambda e: e.tensor_tensor(out=Kt[:], in0=kTt[:, :, tc], in1=Eni[:], op=ALU.mult),
                             reads=["kTt", "Eni"], writes=[("Kt", i % 2)])

                    def s1c(i):
                        Qt = Qts[i % 3]; Kt = Kts[i % 2]; attm = attms[i % 2]
                        patt, akey = pb_rot.next()
                        patt4 = patt[:].rearrange("p (h t) -> p h t", h=4)
                        S.pe([lambda e, h=h: e.matmul(patt4[:, h, :], Kt[:, h, :], Qt[:, h, :], start=True, stop=True)
                              for h in range(4)], reads=[("Kt", i % 2), ("Qt", i % 3)], writes=[akey])
                        S.op("dve", lambda e: e.tensor_tensor(out=attm[:], in0=patt4, in1=maskrep[:, dr, :, :], op=ALU.mult),
                             reads=[akey, "maskrep"], writes=[("attm", i % 2)])

                    def s2(i, part):
                        vb = vbs[i % 4]; ks = kss[i % 3]; Ei = Eis[i % 3]; Qt = Qts[i % 3]; attm = attms[i % 2]
                        p = i % 2
                        po = pos_[p]
                        if part == 0:
                            S.pe([lambda e, h=h: e.matmul(po[:, h, :], vb[:, h * 128:(h + 1) * 128], attm[:, h, :], start=(h == 0), stop=False,
                                                          skip_group_check=True) for h in range(4)],
                                 reads=[("vb", i % 4), ("attm", i % 2)], writes=[("poB", p)])
                            for ci, c in enumerate(corder):
                                cs = slice(c * 64, (c + 1) * 64)
                                pkv = pkvs[ci]
                                S.pe([lambda e, h=h: e.matmul(pkv[:, h * 128:(h + 1) * 128], ks[cs, h * 128:(h + 1) * 128],
                                                              vb[cs, h * 128:(h + 1) * 128], start=True, stop=True) for h in range(4)],
                                     reads=[("ks", i % 3), ("vb", i % 4)], writes=[("pkvB", ci)])
                        cis = [0] if part == 0 else [1]
                        for ci in cis:
                            c = corder[ci]
                            cs = slice(c * 64, (c + 1) * 64)
                            csel = (c * 64 + 63) if dr == 0 else (c * 64)
                            pkv = pkvs[ci]
                            kq = sctr[0]
                            Sr = Sbfs[kq % 2]; Sw = Sbfs[(kq + 1) % 2]
                            S.pe([lambda e, h=h: e.matmul(po[:, h, cs], Sr[:, h * 128:(h + 1) * 128], Qt[:, h, cs], start=False,
                                                          stop=(ci == 1), skip_group_check=True) for h in range(4)],
                                 reads=[("Sbf", kq % 2), ("Qt", i % 3), ("poB", p)], writes=[("poB", p)])
                            St4 = St[:].rearrange("p (h e) -> p h e", h=4)
                            S.op("dve", lambda e: e.tensor_tensor(out=St4, in0=St4,
                                                                  in1=Ei[:, :, csel:csel + 1].broadcast_to([128, 4, 128]), op=ALU.mult),
                                 reads=["St", ("Ei", i % 3)], writes=["St"])
                            S.op("dve", lambda e: e.tensor_tensor(out=St[:], in0=St[:], in1=pkv[:], op=ALU.add),
                                 reads=["St", ("pkvB", ci)], writes=["St"])
                            S.op("act", lambda e: e.activation(out=Sw[:], in_=St[:], func=AF.Copy),
                                 reads=["St", ("Sbf", (kq + 1) % 2)], writes=[("Sbf", (kq + 1) % 2)])
                            sctr[0] += 1

                    def stage3(i):
                        gi_, gq, t = tiles[i]
                        tok0, bidx = ginfo(gi_, gq)
                        p = i % 2
                        po = pos_[p]
                        gp = gi_ % 2
                        tc = slice(t * 128, (t + 1) * 128)
                        if dr == 0:
                            S.op("act", lambda e: e.activation(out=ofwg[:, :, tc], in_=po[:], func=AF.Copy),
                                 reads=[("poB", p)], writes=["ofwg"])
                            if t == torder[-1]:
                                S.dma("pool", OFW_scr[:, :, tok0:tok0 + 512], ofwg[:], reads=["ofwg"], writes=["OFW_scr"])
                        else:
                            osg = osgs[i % 3]; mso = msos[i % 3]
                            S.op("dve", lambda e: e.tensor_tensor(out=osum[:], in0=po[:], in1=ofwts[gp][:, :, tc], op=ALU.add),
                                 reads=[("poB", p), ("ofwt", gp)], writes=["osum"])
                            S.op("act", lambda e: e.activation(out=sqo[:], in_=osum[:], func=AF.Square),
                                 reads=["osum"], writes=["sqo"])
                            S.op("dve", lambda e: e.tensor_tensor(out=osg[:], in0=osum[:], in1=sg2s[gp][:, :, tc], op=ALU.mult),
                                 reads=["osum", ("sg2", gp)], writes=[("osg", i % 3)])
                            pss_, skey = pb_rot.next()
                            pss4 = pss_[:].rearrange("p (h t) -> p h t", h=4)
                            S.pe([lambda e, h=h: e.matmul(pss4[:, h, :], ones_bf[:], sqo[:, h, :], start=True, stop=True)
                                  for h in range(4)], reads=["sqo", "ones_bf"], writes=[skey])
                            S.op("dve", lambda e: e.tensor_scalar(out=mso[:], in0=pss_[:], scalar1=1.0 / 128, scalar2=EPS,
                                                                  op0=ALU.mult, op1=ALU.add), reads=[skey], writes=[("mso", i % 3)])

                    def stage3b(i):
                        if dr == 0:
                            return
                        mso = msos[i % 3]
                        S.op("act", lambda e: e.activation(out=mso[:], in_=mso[:], func=AF.Ln), reads=[("mso", i % 3)], writes=[("mso", i % 3)])
                        S.op("act", lambda e: e.activation(out=mso[:], in_=mso[:], func=AF.Exp, scale=-0.5), reads=[("mso", i % 3)],
                             writes=[("mso", i % 3)])

                    def stage3c(i):
                        if dr == 0:
                            return
                        gi_, gq, t = tiles[i]
                        tok0, bidx = ginfo(gi_, gq)
                        tc = slice(t * 128, (t + 1) * 128)
                        osg = osgs[i % 3]; mso = msos[i % 3]
                        S.op("dve", lambda e: e.tensor_tensor(out=ohg[:, :, tc], in0=osg[:],
                                                              in1=mso[:].rearrange("p (h t) -> p h t", h=4), op=ALU.mult),
                             reads=[("osg", i % 3), ("mso", i % 3)], writes=["ohg"])
                        if t == torder[-1]:
                            S.dma("pool", OH_scr[:, :, tok0:tok0 + 512], ohg[:], reads=["ohg"], writes=["OH_scr"])

                    for i in range(ntl + 6):
                        if 0 <= i - 3 < ntl:
                            s2(i - 3, 0)
                            s2(i - 3, 1)
                        if i < ntl:
                            gi_, gq, t = tiles[i]
                            if t == torder[0]:
                                group_load(gi_, gq)
                            s1a(i)
                        if 0 <= i - 1 < ntl:
                            s1b(i - 1)
                        if i < ntl:
                            if t == torder[0]:
                                group_front(gi_, gq)
                            s1a_ln(i)
                        if 0 <= i - 2 < ntl:
                            s1c(i - 2)
                        if 0 <= i - 4 < ntl:
                            stage3(i - 4)
                        if 0 <= i - 5 < ntl:
                            stage3b(i - 5)
                        if 0 <= i - 6 < ntl:
                            stage3c(i - 6)
            S.barrier()

        sAB.close()

        if stop_after >= 3:
          with ExitStack() as sc:
            sbc = lambda n, s, d=F32: sc.enter_context(nc.sbuf_tensor("C_" + n, list(s), d))
            khs = [sbc("kh%d" % i, [128, 16384], BF16) for i in range(2)]
            vhs = [sbc("vh%d" % i, [128, 128, 65], BF16) for i in range(2)]
            qhs = [sbc("qh%d" % i, [96, 4096], BF16) for i in range(2)]
            NPT = 4
            pTs = [sbc("pT%d" % i, [128, 2, 512], BF16) for i in range(NPT)]
            OT = [sbc("OT%d" % i, [65, 512]) for i in range(2)]
            omT = [sbc("omT%d" % i, [64, 512], BF16) for i in range(2)]
            sel = sbc("sel", [65, 64])
            S.op("pool", lambda e: e.memset(sel[:], 0.0), writes=["sel"])
            S.op("pool", lambda e: e.memset(sel[64:65, :], 1.0), reads=["sel"], writes=["sel"])
            NPS = 3
            pss = [sc.enter_context(nc.psum_tensor("ps%d" % i, [128, 2, 512], F32)) for i in range(NPS)]
            pos = [sc.enter_context(nc.psum_tensor("po%d" % i, [128, 512], F32)) for i in range(1)]
            pbc = sc.enter_context(nc.psum_tensor("pbc", [128, 512], F32))
            seqs = [(0, 64, 0), (64, 128, 4096)]
            units = [(h, sq_) for h in range(8) for sq_ in range(2)]
            if debug == 3:
                units = units[:4]

            def load_unit(ui):
                h, sq_ = units[ui]
                kt0, nkt, _ = seqs[sq_]
                b = ui % 2
                r0 = (h % 2) * 64
                S.dma("sp", khs[b][0:64, 0:nkt * 128], K_scr[r0:r0 + 64, h // 2, kt0 * 128:(kt0 + nkt) * 128],
                      reads=["K_scr"], writes=[("kh", b)])
                S.dma("sp", khs[b][64:96, 0:nkt * 128], KPE_scr[:, kt0 * 128:(kt0 + nkt) * 128],
                      reads=["KPE_scr"], writes=[("kh", b)])
                for c0 in range(0, nkt, 32):
                    S.dma("sp", vhs[b][:, c0:c0 + 32, :], V_scr[h, :, kt0 + c0:kt0 + c0 + 32, :],
                          reads=["V_scr"], writes=[("vh", b)])
                S.dma("sp", qhs[b][:, :], Q_scr[h, :, sq_ * 4096:(sq_ + 1) * 4096], reads=["Q_scr"], writes=[("qh", b)])

            blocks = []
            for ui, (h, sq_) in enumerate(units):
                kt0, nkt, tok0 = seqs[sq_]
                for qg in range(8):
                    for kt in range(0, nkt, 2):
                        blocks.append((ui, h, sq_, qg, kt, nkt))
            LAG = 2
            pending = []

            def emit_qk(bi):
                ui, h, sq_, qg, kt, nkt = blocks[bi]
                b = ui % 2
                ps = pss[bi % NPS]
                S.pe([lambda e, j=j: e.matmul(ps[:, j, :], khs[b][0:96, (kt + j) * 128:(kt + j + 1) * 128],
                                              qhs[b][0:96, qg * 512:(qg + 1) * 512], start=True, stop=True) for j in range(2)],
                     reads=[("kh", b), ("qh", b)], writes=[("ps", bi % NPS)])
                S.op("act", lambda e: e.activation(out=pTs[bi % NPT][:], in_=ps[:], func=AF.Exp),
                     reads=[("ps", bi % NPS)], writes=[("pT", bi % NPT)])

            def emit_pv(bi):
                ui, h, sq_, qg, kt, nkt = blocks[bi]
                b = ui % 2
                gi = (ui * 8 + qg)
                po = pos[0]
                S.pe([lambda e, j=j: e.matmul(po[0:65, :], vhs[b][:, kt + j, :], pTs[bi % NPT][:, j, :], start=(kt + j == 0),
                                              stop=(kt + j == nkt - 1)) for j in range(2)],
                     reads=[("vh", b), ("pT", bi % NPT)], writes=["po"])
                if kt + 2 == nkt:
                    o = OT[gi % 2]; om = omT[gi % 2]
                    tok0 = seqs[sq_][2] + qg * 512
                    S.op("act", lambda e: e.activation(out=o[:], in_=po[0:65, :], func=AF.Copy),
                         reads=["po"], writes=[("OT", gi % 2)])
                    S.op("dve", lambda e: e.reciprocal(out=o[64:65, :], in_=o[64:65, :]),
                         reads=[("OT", gi % 2)], writes=[("OT", gi % 2)])

                    def fin():
                        S.pe([lambda e: e.matmul(pbc[0:64, :], sel[:], o[:], start=True, stop=True)],
                             reads=[("OT", gi % 2), "sel"], writes=["pbc"])
                        S.op("dve", lambda e: e.tensor_tensor(out=om[:], in0=o[0:64, :], in1=pbc[0:64, :], op=ALU.mult),
                             reads=[("OT", gi % 2), "pbc"], writes=[("omT", gi % 2)])
                        S.dma("pool", OM_scr[h, :, tok0:tok0 + 512], om[:], reads=[("omT", gi % 2)], writes=["OM_scr"])
                    pending.append([fin, 4])

            load_unit(0)
            nb = len(blocks)
            for bi in range(nb + LAG):
                if bi < nb:
                    ui, h, sq_, qg, kt, nkt = blocks[bi]
                    if qg == 0 and kt == 2 * LAG and ui + 1 < len(units):
                        load_unit(ui + 1)
                    emit_qk(bi)
                if bi - LAG >= 0:
                    emit_pv(bi - LAG)
                for p_ in list(pending):
                    p_[1] -= 1
                    if p_[1] <= 0:
                        p_[0]()
                        pending.remove(p_)
            for p_ in pending:
                p_[0]()
            S.barrier()


        own_rows = [(8 + i) for i in range(8)] + [(40 + i) for i in range(8)]
        if stop_after >= 4:
          with ExitStack() as sd:
            sbd = lambda n, s_, d=F32: sd.enter_context(nc.sbuf_tensor("D_" + n, list(s_), d))
            Wg = sbd("Wg", [128, 8, 2048], BF16)
            Wb0 = sbd("Wb0", [128, 4, 1024], BF16)
            Wb1 = sbd("Wb1", [64, 8, 1024], BF16)
            Wo = sbd("Wo", [128, 8, 1024], BF16)
            load_w(Wg, w_gates, 8, 2048, "Wg")
            load_w(Wb0, w_branch[0], 4, 1024, "Wb0")
            load_w(Wb1, w_branch[1], 8, 1024, "Wb1", pp=64)
            load_w(Wo, w_out, 8, 1024, "Wo")
            hTd = [sbd("hTd%d" % i, [128, 8, 512], BF16) for i in range(2)]
            ohT = [sbd("ohT%d" % i, [128, 4, 512], BF16) for i in range(2)]
            omT = [sbd("omTd%d" % i, [64, 8, 512], BF16) for i in range(2)]
            xd = sbd("xd", [128, 4, D])
            tg = [sbd("tg%d" % i, [128, 512]) for i in range(2)]
            m1 = sbd("m1", [128, 512]); m2 = sbd("m2", [128, 512])
            mg = sbd("mg", [128, 8, 512], BF16)
            x1o = [sbd("x1o%d" % i, [128, D]) for i in range(2)]
            pbs = [sd.enter_context(nc.psum_tensor("pbD%d" % i, [128, 512], F32)) for i in range(4)]
            pb_rot = Rot(pbs, "pbD")
            px = [sd.enter_context(nc.psum_tensor("pxD%d" % i, [128, 2, 512], F32)) for i in range(2)]
            for gi in range(16):
                b = gi % 2
                tok0 = gi * 512
                S.dma("sp", hTd[b][:], HT_scr[:, :, tok0:tok0 + 512], reads=["HT_scr"], writes=[("hTd", b)])
                S.dma("sp", ohT[b][:], OH_scr[:, :, tok0:tok0 + 512], reads=["OH_scr"], writes=[("ohT", b)])
                S.dma("sp", omT[b][:], OM_scr[:, :, tok0:tok0 + 512].rearrange("h e t -> e h t"), reads=["OM_scr"], writes=[("omTd", b)])
                S.dma("sp", xd[:], x_v[own_rows[gi]], writes=["xd"])
                for j in range(8):
                    for n_ in range(2):
                        pb, key = pb_rot.next()
                        c0 = n_ * 1024 + j * 128
                        S.pe([lambda e, kc=kc, pb=pb, c0=c0: e.matmul(pb[:], Wg[:, kc, c0:c0 + 128], hTd[b][:, kc, :],
                                                                       start=(kc == 0), stop=(kc == 7)) for kc in range(8)],
                             reads=[("hTd", b), "Wg"], writes=[key])
                        S.op("act", lambda e, pb=pb, n_=n_: e.activation(out=tg[n_][:], in_=pb[:], func=AF.Tanh, scale=0.5),
                             reads=[key], writes=[("tg", n_)])
                    pbh, hkey = pb_rot.next()
                    S.pe([lambda e, h=h, pbh=pbh: e.matmul(pbh[:], Wb0[:, h, j * 128:(j + 1) * 128], ohT[b][:, h, :],
                                                           start=(h == 0), stop=(h == 3)) for h in range(4)],
                         reads=[("ohT", b), "Wb0"], writes=[hkey])
                    S.op("dve", lambda e, pbh=pbh: e.scalar_tensor_tensor(out=m1[:], in0=tg[0][:], scalar=1.0, in1=pbh[:],
                                                                          op0=ALU.add, op1=ALU.mult),
                         reads=[("tg", 0), hkey], writes=["m1"])
                    pbm, mkey = pb_rot.next()
                    S.pe([lambda e, h=h, pbm=pbm: e.matmul(pbm[:], Wb1[:, h, j * 128:(j + 1) * 128], omT[b][:, h, :],
                                                           start=(h == 0), stop=(h == 7)) for h in range(8)],
                         reads=[("omTd", b), "Wb1"], writes=[mkey])
                    S.op("dve", lambda e, pbm=pbm: e.scalar_tensor_tensor(out=m2[:], in0=tg[1][:], scalar=1.0, in1=pbm[:],
                                                                          op0=ALU.add, op1=ALU.mult),
                         reads=[("tg", 1), mkey], writes=["m2"])
                    S.op("pool", lambda e, j=j: e.tensor_tensor(out=mg[:, j, :], in0=m1[:], in1=m2[:], op=ALU.add),
                         reads=["m1", "m2"], writes=["mg"])
                for t in range(4):
                    pxt = px[t % 2]
                    tcs = slice(t * 128, (t + 1) * 128)
                    S.pe([lambda e, j=j, nb=nb, pxt=pxt, tcs=tcs: e.matmul(pxt[:, nb, :], mg[:, j, tcs], Wo[:, j, nb * 512:(nb + 1) * 512],
                                                                  start=(j == 0), stop=(j == 7)) for nb in range(2) for j in range(8)],
                         reads=["mg", "Wo"], writes=[("pxD", t % 2)])
                    xo = x1o[t % 2]
                    S.op("dve", lambda e, pxt=pxt, xo=xo, t=t: e.scalar_tensor_tensor(out=xo[:], in0=pxt[:].rearrange("p a b -> p (a b)"),
                                                                           scalar=0.5, in1=xd[:, t, :], op0=ALU.mult, op1=ALU.add),
                         reads=[("pxD", t % 2), "xd"], writes=[("x1o", t % 2)])
                    S.dma("pool", X1_scr[tok0 + t * 128:tok0 + (t + 1) * 128, :], xo[:], reads=[("x1o", t % 2)], writes=["X1_scr"])
            S.barrier()

        if stop_after >= 5:
          with ExitStack() as se:
            sbe = lambda n, s_, d=F32: se.enter_context(nc.sbuf_tensor("E_" + n, list(s_), d))
            Wgu = sbe("Wgu", [128, 8, 2 * DFF], BF16)
            Wd = sbe("Wd", [128, 22, 1024], BF16)
            load_w(Wgu, w_gu, 8, 2 * DFF, "Wgu")
            load_w(Wd, w_down, 22, 1024, "Wd")
            gffn_bc = sbe("gffn_bc", [128, D]); gfin_bc = sbe("gfin_bc", [128, D])
            S.dma("sp", gffn_bc[:], g_ffn.broadcast_to([128, D]), writes=["gffn_bc"])
            S.dma("sp", gfin_bc[:], g_final.broadcast_to([128, D]), writes=["gfin_bc"])
            x1t = [sbe("x1t%d" % i, [128, D]) for i in range(4)]
            hn2 = [sbe("hn2_%d" % i, [128, D], BF16) for i in range(2)]
            h2T = sbe("h2T", [128, 8, 512], BF16)
            actT = sbe("actT", [128, 22, 512], BF16)
            sgt = [sbe("sgt%d" % i, [128, 512]) for i in range(2)]
            ss2 = sbe("ss2", [128, 8]); ms2 = sbe("ms2", [128, 8]); rs2 = sbe("rs2", [128, 8])
            ptr2 = [se.enter_context(nc.psum_tensor("ptrE%d" % i, [128, 8, 128], BF16)) for i in range(2)]
            pbs = [se.enter_context(nc.psum_tensor("pbE%d" % i, [128, 512], F32)) for i in range(4)]
            pb_rot = Rot(pbs, "pbE")
            px2 = se.enter_context(nc.psum_tensor("pxE", [128, 2, 512], F32))
            for gi in range(16):
                tok0 = gi * 512
                for t in range(4):
                    S.dma("sp", x1t[t][:], X1_scr[tok0 + t * 128:tok0 + (t + 1) * 128, :], reads=["X1_scr"], writes=[("x1t", t)])
                    S.op("act", lambda e, t=t: e.activation(out=hn2[t % 2][:], in_=x1t[t][:], func=AF.Square, accum_out=ss2[:, t:t + 1]),
                         reads=[("x1t", t)], writes=[("hn2", t % 2), ("ss2", t)])
                    S.op("dve", lambda e, t=t: e.tensor_scalar(out=ms2[:, t:t + 1], in0=ss2[:, t:t + 1], scalar1=1.0 / D, scalar2=EPS,
                                                               op0=ALU.mult, op1=ALU.add), reads=[("ss2", t)], writes=[("ms2", t)])
                    S.op("pool", lambda e, t=t: e.tensor_tensor(out=rs2[:, t:t + 1], in0=ms2[:, t:t + 1], in1=negh[:, 0:1], op=ALU.pow),
                         reads=[("ms2", t), "negh"], writes=[("rs2", t)])
                    hb = hn2[t % 2]
                    S.op("dve", lambda e, t=t, hb=hb: e.scalar_tensor_tensor(out=hb[:], in0=x1t[t][:], scalar=rs2[:, t:t + 1],
                                                                          in1=gffn_bc[:], op0=ALU.mult, op1=ALU.mult),
                         reads=[("x1t", t), ("rs2", t), "gffn_bc"], writes=[("hn2", t % 2)])
                    ptr = ptr2[t % 2]
                    S.pe([lambda e, kc=kc, ptr=ptr, hb=hb: e.transpose(ptr[:, kc, :], hb[:, kc * 128:(kc + 1) * 128], ident[:])
                          for kc in range(8)], reads=[("hn2", t % 2), "ident"], writes=[("ptrE", t % 2)])
                    S.op("act", lambda e, t=t, ptr=ptr: e.activation(out=h2T[:, :, t * 128:(t + 1) * 128], in_=ptr[:], func=AF.Copy),
                         reads=[("ptrE", t % 2)], writes=["h2T"])
                for fb in range(22):
                    pgt, gkey = pb_rot.next()
                    S.pe([lambda e, kc=kc, pgt=pgt, fb=fb: e.matmul(pgt[:], Wgu[:, kc, fb * 128:(fb + 1) * 128], h2T[:, kc, :],
                                                             start=(kc == 0), stop=(kc == 7)) for kc in range(8)],
                         reads=["h2T", "Wgu"], writes=[gkey])
                    put, ukey = pb_rot.next()
                    S.pe([lambda e, kc=kc, put=put, fb=fb: e.matmul(put[:], Wgu[:, kc, DFF + fb * 128:DFF + (fb + 1) * 128], h2T[:, kc, :],
                                                             start=(kc == 0), stop=(kc == 7)) for kc in range(8)],
                         reads=["h2T", "Wgu"], writes=[ukey])
                    sg_ = sgt[fb % 2]
                    S.op("act", lambda e, pgt=pgt, sg_=sg_: e.activation(out=sg_[:], in_=pgt[:], func=AF.Silu),
                         reads=[gkey], writes=[("sgt", fb % 2)])
                    S.op("dve", lambda e, put=put, sg_=sg_, fb=fb: e.tensor_tensor(out=actT[:, fb, :], in0=sg_[:], in1=put[:], op=ALU.mult),
                         reads=[("sgt", fb % 2), ukey], writes=["actT"])
                for t in range(4):
                    tcs = slice(t * 128, (t + 1) * 128)
                    S.pe([lambda e, fb=fb, nb=nb, tcs=tcs: e.matmul(px2[:, nb, :], actT[:, fb, tcs], Wd[:, fb, nb * 512:(nb + 1) * 512],
                                                           start=(fb == 0), stop=(fb == 21)) for nb in range(2) for fb in range(22)],
                         reads=["actT", "Wd"], writes=["pxE"])
                    yb = x1t[t]
                    S.op("dve", lambda e, t=t: e.tensor_tensor(out=x1t[t][:], in0=px2[:].rearrange("p a b -> p (a b)"), in1=x1t[t][:], op=ALU.add),
                         reads=["pxE", ("x1t", t)], writes=[("x1t", t)])
                    S.op("act", lambda e, t=t: e.activation(out=hn2[t % 2][:], in_=x1t[t][:], func=AF.Square, accum_out=ss2[:, 4 + t:5 + t]),
                         reads=[("x1t", t)], writes=[("hn2", t % 2), ("ss2", 4 + t)])
                    S.op("dve", lambda e, t=t: e.tensor_scalar(out=ms2[:, 4 + t:5 + t], in0=ss2[:, 4 + t:5 + t], scalar1=1.0 / D, scalar2=EPS,
                                                               op0=ALU.mult, op1=ALU.add), reads=[("ss2", 4 + t)], writes=[("ms2", 4 + t)])
                    S.op("pool", lambda e, t=t: e.tensor_tensor(out=rs2[:, 4 + t:5 + t], in0=ms2[:, 4 + t:5 + t], in1=negh[:, 0:1], op=ALU.pow),
                         reads=[("ms2", 4 + t), "negh"], writes=[("rs2", 4 + t)])
                    S.op("dve", lambda e, t=t, yb=yb: e.scalar_tensor_tensor(out=yb[:], in0=x1t[t][:], scalar=rs2[:, 4 + t:5 + t],
                                                                          in1=gfin_bc[:], op0=ALU.mult, op1=ALU.mult),
                         reads=[("x1t", t), ("rs2", 4 + t), "gfin_bc"], writes=[("x1t", t)])
                    S.dma("pool", y[tok0 + t * 128:tok0 + (t + 1) * 128, :], yb[:], reads=[("x1t", t)], writes=["y"])
            S.barrier()

        S.finish()
    return nc, S


SPLITS = np.cumsum([0, 512, 512, 512, 512, 512, 256, 256, 32, 2048])


def _rope_tables(pos, d=32, theta=10000.0):
    inv = theta ** (-np.arange(0, d, 2, dtype=np.float32) / d)
    ang = pos.astype(np.float32)[:, None] * inv[None, :].astype(np.float32)
    return np.cos(ang).astype(np.float32), np.sin(ang).astype(np.float32)


def make_consts():
    s = np.arange(128)[:, None]; t = np.arange(128)[None, :]
    same = (s // 64) == (t // 64)
    Mb_f = (same & (s <= t)).astype(np.float32)
    Mb_b = (same & (s >= t)).astype(np.float32)
    Mks_f = (same & (s > t)).astype(np.float32)
    Mks_b = (same & (s < t)).astype(np.float32)
    return np.stack([Mb_f, Mb_b, Mks_f, Mks_b]).astype(np.float32)


def prep_core(c, inp):
    f32 = np.float32
    p, a = c // 2, c % 2
    s, j = c // 4, c % 4
    xp = inp["x_prompt"]; xs = inp["x_sample"]
    w_in = inp["w_in"][0]
    col = lambda i: w_in[:, SPLITS[i]:SPLITS[i + 1]]
    slots = []
    oth = 1 - a
    pos = np.arange(oth * 4096, (oth + 1) * 4096)
    if a == 0:
        slots.append((xp[p, pos[::-1]], pos[::-1], 1))
    else:
        slots.append((xp[p, pos], pos, 0))
    chain = [(q, 0) for q in range(0, j)] + [(q, 1) for q in range(3, j, -1)]
    for q, d_ in chain:
        pos = np.arange(q * 4096, (q + 1) * 4096)
        if d_ == 1:
            pos = pos[::-1]
        slots.append((xs[s, pos], pos, d_))
    own_p_pos = np.arange(a * 4096, (a + 1) * 4096)
    own_s_pos = np.arange(j * 4096, (j + 1) * 4096)
    x_all = np.concatenate([slots[0][0], xp[p, own_p_pos], slots[1][0], slots[2][0], slots[3][0], xs[s, own_s_pos]], 0)
    pos_all = np.concatenate([slots[0][1], own_p_pos, slots[1][1], slots[2][1], slots[3][1], own_s_pos])
    cosk, sink = _rope_tables(pos_all)
    rope_k = np.stack([np.concatenate([cosk, cosk], 1).T, np.concatenate([sink, sink], 1).T]).astype(f32)
    pos_own = np.concatenate([own_p_pos, own_s_pos])
    cq, sq_ = _rope_tables(pos_own)
    scale = f32(96 ** -0.5)
    Cq = np.concatenate([np.ones((NOWN, 64), f32), cq, cq], 1).T * scale
    Sq = np.concatenate([np.zeros((NOWN, 64), f32), sq_, sq_], 1).T * scale
    rope_q = np.stack([Cq, Sq]).astype(f32)
    w_ctx = np.stack([np.concatenate([col(1), col(2 + d_)], 1) for (_, _, d_) in slots]).astype(f32)
    lbp_ctx = np.stack([inp["lb_param"][:, d_, :] for (_, _, d_) in slots]).astype(f32)
    flags = np.zeros((128, 16), f32)
    flags[:, 0] = 1.0 if a == 1 else 0.0
    flags[:, 1] = 1.0 if a == 0 else 0.0
    for k in (2, 3):
        flags[:, 2 + k] = 0.0 if (k - 1) == j else 1.0
    for k in (1, 2, 3):
        flags[:, 5 + k] = 1.0 if k == j else 0.0
    flags[:, 9] = 1.0 if j < 3 else 0.0
    w_ukv = inp["w_ukv"][0]
    d = {
        "x_all": x_all.astype(f32),
        "g_mix": inp["g_mix"].reshape(1, D),
        "w_kv": np.concatenate([col(6), col(7)], 1),
        "w_q": col(5),
        "w_ctx": w_ctx,
        "w_hg": np.concatenate([col(0), col(1), col(2), col(3), col(4)], 1),
        "w_gates": col(8),
        "w_uq": inp["w_uq"][0].reshape(256, 768),
        "w_uk": w_ukv[:, :, :64].reshape(256, 512),
        "w_uv": w_ukv[:, :, 64:].reshape(256, 512),
        "lbp_ctx": lbp_ctx,
        "lbp_own": inp["lb_param"].reshape(1, 2048),
        "g_onorm": inp["g_onorm"].reshape(1, 512),
        "g_qa": inp["g_qa"].reshape(2, 128).T,
        "g_kva": inp["g_kva"].reshape(2, 128).T,
        "w_branch": inp["w_branch"][0],
        "w_out": inp["w_out"][0],
        "g_ffn": inp["g_ffn"].reshape(1, D),
        "w_gu": inp["w_gate_up"][0],
        "w_down": inp["w_down"][0],
        "g_final": inp["g_final"].reshape(1, D),
        "cmat": make_consts(),
        "rope_q": rope_q,
        "rope_k": rope_k,
        "flags": flags,
        "lbc": inp["lb_param"].reshape(2, 2, 4, 128).transpose(3, 0, 1, 2).reshape(128, 16),
        "gon_c": inp["g_onorm"].reshape(4, 128).T,
        "maskrep": np.repeat(make_consts()[0:2].transpose(1, 0, 2)[:, :, None, :], 4, axis=2),
    }
    return {k: np.ascontiguousarray(v, dtype=np.float32) for k, v in d.items()}


_CACHE = {}


def kernel(**inputs):
    inp = {k: np.asarray(v) for k, v in inputs.items()}
    if "nc" not in _CACHE:
        _CACHE["nc"] = build_nc()[0]
    nc = _CACHE["nc"]
    in_maps = [prep_core(c, inp) for c in range(8)]
    res = run_bass_kernel_spmd(nc, in_maps, core_ids=list(range(8)))
    yp = np.zeros((4, 8192, D), np.float32)
    ys = np.zeros((2, 16384, D), np.float32)
    for c in range(8):
        yc = res.results[c]["y"]
        p, a = c // 2, c % 2
        s, j = c // 4, c % 4
        yp[p, a * 4096:(a + 1) * 4096] = yc[:4096]
        ys[s, j * 4096:(j + 1) * 4096] = yc[4096:]
    return (yp, ys)
```
